# Optimizing a Trainium2 kernel written in Bass

```python
import math, functools
import jax, jax.numpy as jnp
from jax import lax
import numpy as np

D_MODEL = 1024
BATCH = 2
SEQ = 8192
DEPTH = 4
DEC_BATCH = 128
DEC_SEQ = 4
PAST_LEN = 8192
PAGE_SIZE = 128

WINDOW = 128
ATTN_BLOCK = WINDOW
N_Q_HEADS = 8
N_KV_HEADS = 2
HEAD_DIM = 64
Q_GROUP = N_Q_HEADS // N_KV_HEADS
A_WIDTH = N_Q_HEADS * HEAD_DIM
KV_WIDTH = N_KV_HEADS * HEAD_DIM
SC_WIDTH = 512
SC_CONV = 3
DN_HEADS = 4
DN_DK = 128
DN_DV = 128
DN_CONV = 4
DN_CHUNK = 64
DN_QK_WIDTH = DN_HEADS * DN_DK
DN_V_WIDTH = DN_HEADS * DN_DV
DN_QKV = 2 * DN_QK_WIDTH + DN_V_WIDTH
D_FF = 2816
FF_CONV = 3
N_BRANCH = 3
RMS_EPS = 1e-6
L2_EPS = 1e-6
IN_SIZES = (A_WIDTH, KV_WIDTH, KV_WIDTH, SC_WIDTH, SC_WIDTH, SC_WIDTH, DN_QKV, DN_V_WIDTH, DN_HEADS, DN_HEADS, N_BRANCH * D_MODEL)
IN_COLS = sum(IN_SIZES)

kernel_name = "hybrid_gated_parallel_swa_shortconv_gdn_step"


def rmsnorm(x, w):
    xf = x.astype(jnp.float32)
    y = xf * lax.rsqrt(jnp.mean(xf * xf, axis=-1, keepdims=True) + RMS_EPS)
    return (y * w.astype(jnp.float32)).astype(x.dtype)


def l2norm(x):
    return x * lax.rsqrt(jnp.sum(x * x, axis=-1, keepdims=True) + L2_EPS)


def causal_dwconv(x, prev, w):
    K = w.shape[0]
    L = x.shape[1]
    xp = jnp.concatenate([prev.astype(x.dtype), x], axis=1)
    y = xp[:, 0:L] * w[0]
    for j in range(1, K):
        y = y + xp[:, j:j + L] * w[j]
    return y, xp[:, L:]


def sink_attention(q, k, v, mask, sinks):
    s = jnp.einsum('bnqhgd,bnkhd->bnhgqk', q, k, preferred_element_type=jnp.float32) * (HEAD_DIM ** -0.5)
    s = jnp.where(mask[None, :, None, None], s, -jnp.inf)
    sink = sinks.astype(jnp.float32)[None, None, :, :, None, None]
    m = jnp.maximum(jnp.max(s, axis=-1, keepdims=True), sink)
    p = jnp.exp(s - m)
    denom = jnp.sum(p, axis=-1, keepdims=True) + jnp.exp(sink - m)
    p = (p / denom).astype(v.dtype)
    return jnp.einsum('bnhgqk,bnkhd->bnqhgd', p, v)


def swa_prompt(q, k, v, sinks):
    B, L = q.shape[:2]
    nb = L // ATTN_BLOCK
    qb = q.reshape(B, nb, ATTN_BLOCK, N_KV_HEADS, Q_GROUP, HEAD_DIM)

    def band(t):
        cur = t.reshape(B, nb, ATTN_BLOCK, N_KV_HEADS, HEAD_DIM)
        prev = jnp.pad(cur[:, :-1], ((0, 0), (1, 0), (0, 0), (0, 0), (0, 0)))
        return jnp.concatenate([prev, cur], axis=2)

    kb, vb = band(k), band(v)
    blk = jnp.arange(nb)[:, None] * ATTN_BLOCK
    qpos = blk + jnp.arange(ATTN_BLOCK)[None]
    kpos = blk - ATTN_BLOCK + jnp.arange(2 * ATTN_BLOCK)[None]
    d = qpos[:, :, None] - kpos[:, None, :]
    mask = (kpos[:, None, :] >= 0) & (d >= 0) & (d <= WINDOW)
    o = sink_attention(qb, kb, vb, mask, sinks.reshape(N_KV_HEADS, Q_GROUP))
    return o.reshape(B, L, A_WIDTH)


def swa_sample(q, k, v, sinks, ck, cv):
    B, L = q.shape[:2]
    W = ck.shape[1]
    kk = jnp.concatenate([ck.astype(k.dtype), k], axis=1)
    vv = jnp.concatenate([cv.astype(v.dtype), v], axis=1)
    qpos = PAST_LEN + jnp.arange(L)
    kpos = PAST_LEN - W + jnp.arange(W + L)
    d = qpos[:, None] - kpos[None, :]
    mask = ((d >= 0) & (d <= WINDOW))[None]
    o = sink_attention(q.reshape(B, 1, L, N_KV_HEADS, Q_GROUP, HEAD_DIM), kk[:, None], vv[:, None], mask,
                       sinks.reshape(N_KV_HEADS, Q_GROUP))
    return o.reshape(B, L, A_WIDTH), kk[:, -W:], vv[:, -W:]


def chunk_gated_delta(q, k, v, g, beta, S0):
    B, L, H, dk = q.shape
    dv = v.shape[-1]
    C = DN_CHUNK if L >= DN_CHUNK else L
    pad = (-L) % C
    if pad:
        q, k, v = [jnp.pad(t, ((0, 0), (0, pad), (0, 0), (0, 0))) for t in (q, k, v)]
        g, beta = [jnp.pad(t, ((0, 0), (0, pad), (0, 0))) for t in (g, beta)]
    n = (L + pad) // C

    def to_chunks(t):
        return t.reshape(B, n, C, H, t.shape[-1]).transpose(1, 0, 3, 2, 4)

    q, k, v = to_chunks(q), to_chunks(k), to_chunks(v)
    g = g.reshape(B, n, C, H).transpose(1, 0, 3, 2)
    beta = beta.reshape(B, n, C, H).transpose(1, 0, 3, 2)
    gc = jnp.cumsum(g, axis=-1)
    incl = jnp.tril(jnp.ones((C, C), dtype=bool))
    strict = jnp.tril(jnp.ones((C, C), dtype=bool), -1)
    decay = jnp.exp(jnp.where(incl, gc[..., :, None] - gc[..., None, :], -jnp.inf))
    kb = k * beta[..., None]
    vb = v * beta[..., None]
    a_low = jnp.where(strict, jnp.einsum('nbhid,nbhjd->nbhij', kb, k) * decay, 0.0)
    lhs = a_low + jnp.eye(C, dtype=jnp.float32)
    rhs = jnp.concatenate([vb, kb * jnp.exp(gc)[..., None]], axis=-1)
    sol = lax.linalg.triangular_solve(lhs, rhs, left_side=True, lower=True, unit_diagonal=True)
    u, w = sol[..., :dv], sol[..., dv:]
    qk = jnp.einsum('nbhid,nbhjd->nbhij', q, k) * decay

    def step(S, inp):
        qc, kc, uc, wc, gcc, qkc = inp
        v_new = uc - jnp.einsum('bhcd,bhde->bhce', wc, S)
        o = jnp.einsum('bhcd,bhde->bhce', qc * jnp.exp(gcc)[..., None], S) + jnp.einsum('bhij,bhje->bhie', qkc, v_new)
        glast = gcc[..., -1]
        S = S * jnp.exp(glast)[..., None, None] + jnp.einsum(
            'bhcd,bhce->bhde', kc * jnp.exp(glast[..., None] - gcc)[..., None], v_new)
        return S, o

    S, o = lax.scan(step, S0, (q, k, u, w, gc, qk))
    o = o.transpose(1, 0, 3, 2, 4).reshape(B, n * C, H, dv)[:, :L]
    return o, S


def gated_deltanet(qkv_raw, z, a, b, conv_prev, S0, conv_w, A_log, dt_bias, norm_w):
    f32 = jnp.float32
    B, L = qkv_raw.shape[:2]
    qkv, conv_new = causal_dwconv(qkv_raw, conv_prev, conv_w)
    qkv = jax.nn.silu(qkv).astype(f32)
    q, k, v = jnp.split(qkv, [DN_QK_WIDTH, 2 * DN_QK_WIDTH], axis=-1)
    q = l2norm(q.reshape(B, L, DN_HEADS, DN_DK)) * (DN_DK ** -0.5)
    k = l2norm(k.reshape(B, L, DN_HEADS, DN_DK))
    v = v.reshape(B, L, DN_HEADS, DN_DV)
    beta = jax.nn.sigmoid(b.astype(f32))
    g = -jnp.exp(A_log.astype(f32)) * jax.nn.softplus(a.astype(f32) + dt_bias.astype(f32))
    o, S = chunk_gated_delta(q, k, v, g, beta, S0.astype(f32))
    o = o * lax.rsqrt(jnp.mean(o * o, axis=-1, keepdims=True) + RMS_EPS) * norm_w.astype(f32)
    o = o * jax.nn.silu(z.astype(f32).reshape(B, L, DN_HEADS, DN_DV))
    return o.reshape(B, L, DN_V_WIDTH).astype(qkv_raw.dtype), conv_new, S.astype(S0.dtype)


def decoder_layer(x, ln1, w_in, sinks, sc_conv_w, dn_conv_w, dn_A_log, dn_dt_bias, dn_norm_w,
                  w_br_a, w_br_b, w_br_c, w_o, ln2, ffn_up, ffn_conv_w, ffn_down,
                  attn_fn, sc_prev, dnc_prev, dn_S0, ff_prev):
    B, L = x.shape[:2]
    h = rmsnorm(x, ln1)
    proj = h @ w_in
    split_at = np.cumsum(IN_SIZES)[:-1].tolist()
    aq, ak, av, sh, sB, sC, dqkv, dz, da, db, gl = jnp.split(proj, split_at, axis=-1)
    oa, new_k, new_v = attn_fn(aq.reshape(B, L, N_Q_HEADS, HEAD_DIM), ak.reshape(B, L, N_KV_HEADS, HEAD_DIM),
                               av.reshape(B, L, N_KV_HEADS, HEAD_DIM), sinks)
    cu, sc_new = causal_dwconv(sC * sh, sc_prev, sc_conv_w)
    ob = sB * cu
    oc, dnc_new, dn_S = gated_deltanet(dqkv, dz, da, db, dnc_prev, dn_S0, dn_conv_w, dn_A_log, dn_dt_bias, dn_norm_w)
    gates = jax.nn.sigmoid(gl.astype(jnp.float32)).astype(x.dtype).reshape(B, L, N_BRANCH, D_MODEL)
    merged = gates[:, :, 0] * (oa @ w_br_a) + gates[:, :, 1] * (ob @ w_br_b) + gates[:, :, 2] * (oc @ w_br_c)
    x = x + merged @ w_o
    up = rmsnorm(x, ln2) @ ffn_up
    upc, ff_new = causal_dwconv(up, ff_prev, ffn_conv_w)
    gate, val = jnp.split(upc, 2, axis=-1)
    x = x + (jax.nn.silu(gate) * val) @ ffn_down
    return x, (new_k, new_v, sc_new, dnc_new, dn_S, ff_new)


def setup_inputs(seed: int = 0) -> dict:
    key = jax.random.key(seed)
    ks = iter(jax.random.split(key, 32))
    f32 = jnp.float32

    def nrm(shape, scale):
        return scale * jax.random.normal(next(ks), shape, f32)

    win_buf = min(WINDOW, PAST_LEN)
    x_prompt = nrm((BATCH, SEQ, D_MODEL), 1.0)
    x_sample = nrm((DEC_BATCH, DEC_SEQ, D_MODEL), 1.0)
    cache_swa_k = nrm((DEPTH, DEC_BATCH, win_buf, N_KV_HEADS, HEAD_DIM), 1.0)
    cache_swa_v = nrm((DEPTH, DEC_BATCH, win_buf, N_KV_HEADS, HEAD_DIM), 1.0)
    state_sc_conv = nrm((DEPTH, DEC_BATCH, SC_CONV - 1, SC_WIDTH), 1.0)
    state_dn_conv = nrm((DEPTH, DEC_BATCH, DN_CONV - 1, DN_QKV), 1.0)
    state_dn = nrm((DEPTH, DEC_BATCH, DN_HEADS, DN_DK, DN_DV), 0.5)
    state_ffn_conv = nrm((DEPTH, DEC_BATCH, FF_CONV - 1, 2 * D_FF), 1.0)
    ln1 = 1.0 + nrm((DEPTH, D_MODEL), 0.02)
    w_in = nrm((DEPTH, D_MODEL, IN_COLS), D_MODEL ** -0.5)
    attn_sinks = nrm((DEPTH, N_Q_HEADS), 1.0)
    sc_conv_w = nrm((DEPTH, SC_CONV, SC_WIDTH), SC_CONV ** -0.5)
    dn_conv_w = nrm((DEPTH, DN_CONV, DN_QKV), DN_CONV ** -0.5)
    dn_A_log = jnp.log(jax.random.uniform(next(ks), (DEPTH, DN_HEADS), f32, 1.0, 16.0))
    dt = jnp.exp(jax.random.uniform(next(ks), (DEPTH, DN_HEADS), f32, math.log(1e-3), math.log(1e-1)))
    dn_dt_bias = dt + jnp.log(-jnp.expm1(-dt))
    dn_norm_w = 1.0 + nrm((DEPTH, DN_DV), 0.02)
    w_br_a = nrm((DEPTH, A_WIDTH, D_MODEL), A_WIDTH ** -0.5)
    w_br_b = nrm((DEPTH, SC_WIDTH, D_MODEL), SC_WIDTH ** -0.5)
    w_br_c = nrm((DEPTH, DN_V_WIDTH, D_MODEL), DN_V_WIDTH ** -0.5)
    w_o = nrm((DEPTH, D_MODEL, D_MODEL), D_MODEL ** -0.5)
    ln2 = 1.0 + nrm((DEPTH, D_MODEL), 0.02)
    ffn_up = nrm((DEPTH, D_MODEL, 2 * D_FF), D_MODEL ** -0.5)
    ffn_conv_w = nrm((DEPTH, FF_CONV, 2 * D_FF), FF_CONV ** -0.5)
    ffn_down = nrm((DEPTH, D_FF, D_MODEL), D_FF ** -0.5)
    ln_f = 1.0 + nrm((D_MODEL,), 0.02)
    return {"x_prompt": x_prompt, "x_sample": x_sample,
            "cache_swa_k": cache_swa_k, "cache_swa_v": cache_swa_v,
            "state_sc_conv": state_sc_conv, "state_dn_conv": state_dn_conv,
            "state_dn": state_dn, "state_ffn_conv": state_ffn_conv,
            "ln1": ln1, "w_in": w_in, "attn_sinks": attn_sinks, "sc_conv_w": sc_conv_w,
            "dn_conv_w": dn_conv_w, "dn_A_log": dn_A_log, "dn_dt_bias": dn_dt_bias, "dn_norm_w": dn_norm_w,
            "w_br_a": w_br_a, "w_br_b": w_br_b, "w_br_c": w_br_c, "w_o": w_o,
            "ln2": ln2, "ffn_up": ffn_up, "ffn_conv_w": ffn_conv_w, "ffn_down": ffn_down, "ln_f": ln_f}


def reference(x_prompt, x_sample, cache_swa_k, cache_swa_v, state_sc_conv, state_dn_conv, state_dn, state_ffn_conv,
              ln1, w_in, attn_sinks, sc_conv_w, dn_conv_w, dn_A_log, dn_dt_bias, dn_norm_w,
              w_br_a, w_br_b, w_br_c, w_o, ln2, ffn_up, ffn_conv_w, ffn_down, ln_f):
    W = cache_swa_k.shape[2]
    Bp = x_prompt.shape[0]
    dt = x_prompt.dtype

    def attn_prompt(q, k, v, s):
        return swa_prompt(q, k, v, s), k[:, -W:], v[:, -W:]

    xp, xs = x_prompt, x_sample
    st_p, st_s = [], []
    for l in range(DEPTH):
        lw = (ln1[l], w_in[l], attn_sinks[l], sc_conv_w[l], dn_conv_w[l], dn_A_log[l], dn_dt_bias[l], dn_norm_w[l],
              w_br_a[l], w_br_b[l], w_br_c[l], w_o[l], ln2[l], ffn_up[l], ffn_conv_w[l], ffn_down[l])
        xp, sp = decoder_layer(xp, *lw, attn_prompt,
                               jnp.zeros((Bp, SC_CONV - 1, SC_WIDTH), dt),
                               jnp.zeros((Bp, DN_CONV - 1, DN_QKV), dt),
                               jnp.zeros((Bp, DN_HEADS, DN_DK, DN_DV), dt),
                               jnp.zeros((Bp, FF_CONV - 1, 2 * D_FF), dt))
        xs, ss = decoder_layer(xs, *lw, functools.partial(swa_sample, ck=cache_swa_k[l], cv=cache_swa_v[l]),
                               state_sc_conv[l], state_dn_conv[l], state_dn[l], state_ffn_conv[l])
        st_p.append(sp)
        st_s.append(ss)
    y_prompt = rmsnorm(xp, ln_f)
    y_sample = rmsnorm(xs, ln_f)
    kp, vp, scp, dncp, dnp, ffp = [jnp.stack(t) for t in zip(*st_p)]
    ks_, vs_, scs, dncs, dns, ffs = [jnp.stack(t) for t in zip(*st_s)]
    return (y_prompt, y_sample, kp, vp, scp, dncp, dnp, ffp, ks_, vs_, scs, dncs, dns, ffs)
```

```python
import numpy as np
from contextlib import ExitStack
import concourse.bass as bass
import concourse.mybir as mybir
from concourse.bass_utils import run_bass_kernel_spmd

F32 = mybir.dt.float32
BF16 = mybir.dt.bfloat16
AF = mybir.ActivationFunctionType
ALU = mybir.AluOpType
AX = mybir.AxisListType

D = 1024
NCORE = 8
NSQ = 16
NS = 64
NPB = 1024
HALO = 128
DFF = 2816
INC = 7432
Q0, K0, V0, SH0, SB0, SC0, DN0, Z0, A0, B0, G0 = 0, 512, 640, 768, 1280, 1792, 2304, 3840, 4352, 4356, 4360
NEG = -30000.0


class KB:
    def __init__(self):
        self.nc = bass.Bass("TRN2", target_bir_lowering=False)
        self.es = ExitStack()
        nc = self.nc
        self.E = dict(pe=nc.tensor, act=nc.scalar, dve=nc.vector, pool=nc.gpsimd, sp=nc.sync)
        self.semobj = {}
        self.ccnt = {}
        for e in ('pe', 'act', 'dve', 'pool'):
            self.semobj['c_' + e] = self.es.enter_context(nc.semaphore('c_' + e))
            self.ccnt[e] = 0
        self.dq = {}
        for q in ('sp', 'pool'):
            ids = []
            for i in range(16):
                sid = 'd_%s%d' % (q, i)
                self.semobj[sid] = self.es.enter_context(nc.semaphore(sid))
                ids.append(sid)
            self.dq[q] = [ids, 0]
        self.dcnt = {}
        self.waited = {e: {} for e in self.E}
        self.lastw = {}
        self.readers = {}
        self.ninst = 0
        self.cur = self.es
        self.pend = {}
        self.psum_names = set()
        self.seen = set()
        self.pnames = None
        self.uid = 0

    def sb(self, name, shape, dt=F32):
        if self.pnames is not None:
            self.uid += 1
            name = "%s_u%d" % (name, self.uid)
            self.pnames[-1].append(name)
        return self.cur.enter_context(self.nc.sbuf_tensor(name, shape, dt))

    def phase(self):
        kb = self

        class _P:
            def __enter__(s):
                s.st = ExitStack(); s.prev = kb.cur; kb.cur = s.st
                if kb.pnames is None:
                    kb.pnames = []
                kb.pnames.append([])
                return s

            def __exit__(s, *a):
                names = kb.pnames.pop()
                if not kb.pnames:
                    kb.pnames = None
                for nm in names:
                    t = kb.lastw.pop(nm, None)
                    if t:
                        kb.pend[t[0]] = max(kb.pend.get(t[0], 0), t[1])
                    for sid, v in kb.readers.pop(nm, {}).items():
                        kb.pend[sid] = max(kb.pend.get(sid, 0), v)
                kb.cur = s.prev
                s.st.close()
                return False
        return _P()

    def ps(self, name, shape, dt=F32):
        self.psum_names.add(name)
        return self.es.enter_context(self.nc.psum_tensor(name, shape, dt))

    @staticmethod
    def _keys(aps):
        ks = []
        for a in aps:
            if a is None or isinstance(a, (int, float)):
                continue
            ks.append(a.tensor.name)
        return ks

    def emit(self, e, fn, reads, writes, dma=False):
        reads = self._keys(reads)
        writes = self._keys(writes)
        need = {}
        for k in reads:
            t = self.lastw.get(k)
            if t:
                need[t[0]] = max(need.get(t[0], 0), t[1])
        for k in writes:
            t = self.lastw.get(k)
            if t:
                need[t[0]] = max(need.get(t[0], 0), t[1])
            for sid, v in self.readers.get(k, {}).items():
                need[sid] = max(need.get(sid, 0), v)
        for k in reads:
            if k in self.psum_names:
                for sid, v in self.readers.get(k, {}).items():
                    if sid != 'c_' + e:
                        need[sid] = max(need.get(sid, 0), v)
        for k in reads + writes:
            if k not in self.seen:
                self.seen.add(k)
                for sid, v in self.pend.items():
                    need[sid] = max(need.get(sid, 0), v)
        for sid, v in need.items():
            if e == 'pe' and sid == 'c_pe':
                continue
            if self.waited[e].get(sid, 0) < v:
                self.E[e].wait_ge(self.semobj[sid], v)
                self.waited[e][sid] = v
        if dma:
            ids, rr = self.dq[e]
            sid = ids[rr % len(ids)]
            self.dq[e][1] = rr + 1
            if self.dcnt.get(sid, 0) > self.waited[e].get(sid, 0):
                self.E[e].wait_ge(self.semobj[sid], self.dcnt[sid])
                self.waited[e][sid] = self.dcnt[sid]
        ins = fn()
        self.ninst += 1
        if dma:
            self.dcnt[sid] = self.dcnt.get(sid, 0) + 16
            ins.then_inc(self.semobj[sid], 16)
            tok = (sid, self.dcnt[sid])
        else:
            self.ccnt[e] += 1
            ins.then_inc(self.semobj['c_' + e], 1)
            tok = ('c_' + e, self.ccnt[e])
        for k in reads:
            r = self.readers.setdefault(k, {})
            r[tok[0]] = max(r.get(tok[0], 0), tok[1])
        for k in writes:
            self.lastw[k] = tok
            self.readers[k] = {}
        return ins

    def mm(self, out, lhsT, rhs, start=True, stop=True):
        return self.emit('pe', lambda: self.nc.tensor.matmul(out, lhsT, rhs, start=start, stop=stop), [lhsT, rhs], [out])

    def tr(self, out, in_, ident):
        return self.emit('pe', lambda: self.nc.tensor.transpose(out, in_, ident), [in_, ident], [out])

    def act(self, out, in_, func, bias=None, scale=None, accum=None):
        kw = {}
        if bias is not None:
            kw['bias'] = bias
        if scale is not None:
            kw['scale'] = scale
        if accum is not None:
            kw['accum_out'] = accum
        return self.emit('act', lambda: self.nc.scalar.activation(out=out, in_=in_, func=func, **kw),
                         [in_, bias, scale], [out, accum])

    def tt(self, out, a, b, op, eng='dve'):
        return self.emit(eng, lambda: self.E[eng].tensor_tensor(out=out, in0=a, in1=b, op=op), [a, b], [out])

    def ts(self, out, a, s1, s2=None, op0=ALU.mult, op1=None, eng='dve'):
        kw = {}
        if op1 is not None:
            kw['op1'] = op1
        return self.emit(eng, lambda: self.E[eng].tensor_scalar(out=out, in0=a, scalar1=s1, scalar2=s2, op0=op0, **kw),
                         [a, s1, s2], [out])

    def stt(self, out, a, scalar, b, op0, op1, eng='dve'):
        return self.emit(eng, lambda: self.E[eng].scalar_tensor_tensor(out=out, in0=a, scalar=scalar, in1=b, op0=op0, op1=op1),
                         [a, scalar, b], [out])

    def cp(self, out, in_, eng='dve'):
        return self.emit(eng, lambda: self.E[eng].tensor_copy(out=out, in_=in_), [in_], [out])

    def rmax(self, out, in_):
        return self.emit('dve', lambda: self.nc.vector.reduce_max(out=out, in_=in_, axis=AX.X), [in_], [out])

    def rsum(self, out, in_):
        return self.emit('dve', lambda: self.nc.vector.reduce_sum(out=out, in_=in_, axis=AX.X), [in_], [out])

    def rcp(self, out, in_):
        return self.emit('dve', lambda: self.nc.vector.reciprocal(out=out, in_=in_), [in_], [out])

    def memset(self, ap, val, eng='dve'):
        return self.emit(eng, lambda: self.E[eng].memset(ap, val), [], [ap])

    def dma(self, out, in_, q='sp'):
        return self.emit(q, lambda: self.E[q].dma_start(out=out, in_=in_), [in_], [out], dma=True)

    def finish(self):
        for sid, v in list(self.dcnt.items()):
            self.nc.sync.wait_ge(self.semobj[sid], v)
        for e, v in self.ccnt.items():
            if v:
                self.nc.sync.wait_ge(self.semobj['c_' + e], v)


def build(SEQ, L):
    import os
    STOP = float(os.environ.get('KSTOP', '99'))
    KVAR = int(os.environ.get('KVAR', '0'))
    NBLK = SEQ // NPB
    kb = KB()
    nc = kb.nc

    def din(name, shape, dt=F32):
        return nc.dram_tensor(name, list(shape), dt, kind="ExternalInput").ap()

    def dout(name, shape):
        return nc.dram_tensor(name, list(shape), F32, kind="ExternalOutput").ap()

    xp = din("xp", [SEQ, D]); xsm = din("xsm", [NS, D])
    ck = din("ck", [L, NSQ, 128, 128]); cv = din("cv", [L, NSQ, 128, 128])
    st_sc = din("st_sc", [L, NSQ * 2, 512]); st_dnc = din("st_dnc", [L, NSQ * 3, 1536])
    st_dn = din("st_dn", [L, NSQ, 4, 128, 128]); st_ff = din("st_ff", [L, NSQ * 2, 2 * DFF])
    ln1 = din("ln1", [L, 8, 128]); w_in = din("w_in", [L, D, INC]); sinks = din("sinks", [L, 8])
    scw = din("scw", [L, 3, 512]); dncw = din("dncw", [L, 4, 1536]); alog = din("alog", [L, 4]); dtb = din("dtb", [L, 4])
    dnw = din("dnw", [L, 128]); wba = din("wba", [L, 512, D]); wbb = din("wbb", [L, 512, D]); wbc = din("wbc", [L, 512, D])
    wo = din("wo", [L, D, D]); ln2 = din("ln2", [L, 8, 128]); fup = din("fup", [L, D, 2 * DFF]); fcw = din("fcw", [L, 3, 2 * DFF])
    fdn = din("fdn", [L, DFF, D]); lnf = din("lnf", [8, 128])
    c_id = din("c_id", [128, 128]); c_mp = din("c_mp", [128, 256]); c_mp0 = din("c_mp0", [128, 256]); c_ms = din("c_ms", [4, 132])
    c_su = din("c_su", [128, 128]); c_sl = din("c_sl", [128, 128]); c_iu = din("c_iu", [128, 128])
    c_su4 = din("c_su4", [64, 64]); c_sl4 = din("c_sl4", [64, 64]); c_iu4 = din("c_iu4", [64, 64])
    c_bi = din("c_bi", [64, 16]); c_bm = din("c_bm", [128, 16 * 64])
    c_bo = din("c_bo", [128, 128]); c_bo4 = din("c_bo4", [64, 64])
    c_mniu = din("c_mniu", [128, 128]); c_mnil = din("c_mnil", [128, 128]); c_mniu4 = din("c_mniu4", [64, 64]); c_mnil4 = din("c_mnil4", [64, 64])

    yp = dout("yp", [SEQ, D]); ys = dout("ys", [NS, D])
    o_kp = dout("o_kp", [L, 128, 128]); o_vp = dout("o_vp", [L, 128, 128])
    o_scp = dout("o_scp", [L, 2, 512]); o_dncp = dout("o_dncp", [L, 3, 1536]); o_dnp = dout("o_dnp", [L, 4, 128, 128])
    o_ffp = dout("o_ffp", [L, 2, 2 * DFF])
    o_ks = dout("o_ks", [L, NSQ, 128, 128]); o_vs = dout("o_vs", [L, NSQ, 128, 128])
    o_scs = dout("o_scs", [L, NSQ, 2, 512]); o_dncs = dout("o_dncs", [L, NSQ, 3, 1536]); o_dns = dout("o_dns", [L, NSQ, 4, 128, 128])
    o_ffs = dout("o_ffs", [L, NSQ, 2, 2 * DFF])

    NTOT = SEQ + NS
    xbuf = [nc.dram_tensor("xbuf%d" % i, [128, 8, NTOT], F32).ap() for i in range(2)]

    sb, ps = kb.sb, kb.ps
    HC = HALO + NPB + NS
    XC = NPB + NS
    idf = sb("idf", [128, 128]); idb = sb("idb", [128, 128], BF16); onb = sb("onb", [128, 128], BF16); onf = sb("onf", [128, 128])
    mp = sb("mp", [128, 256]); mp0 = sb("mp0", [128, 256]); msm = sb("msm", [4, 132])
    su = sb("su", [128, 128]); sl = sb("sl", [128, 128]); iu = sb("iu", [128, 128])
    su4 = sb("su4", [64, 64]); sl4 = sb("sl4", [64, 64]); iu4 = sb("iu4", [64, 64]); bi = sb("bi", [64, 16]); bm = sb("bm", [128, 16, 64])
    bo = sb("bo", [128, 128]); bo4 = sb("bo4", [64, 64]); mniu = sb("mniu", [128, 128]); mnil = sb("mnil", [128, 128])
    mniu4 = sb("mniu4", [64, 64]); mnil4 = sb("mnil4", [64, 64])
    for t_, c_ in ((bo, c_bo), (bo4, c_bo4), (mniu, c_mniu), (mnil, c_mnil), (mniu4, c_mniu4), (mnil4, c_mnil4), (idf, c_id), (mp, c_mp), (mp0, c_mp0), (msm, c_ms), (su, c_su), (sl, c_sl), (iu, c_iu), (su4, c_su4),
                   (sl4, c_sl4), (iu4, c_iu4), (bi, c_bi)):
        kb.dma(t_[:], c_)
    kb.dma(bm[:], c_bm.rearrange("p (s t) -> p s t", s=16))
    kb.dma(idb[:], c_id, q='pool')
    kb.memset(onb[:], 1.0); kb.memset(onf[:], 1.0)
    epsc = sb("epsc", [128, 1]); kb.memset(epsc[:], 1e-6)
    sq = [sb("sq%d" % i, [128, 512], BF16) for i in range(2)]
    rstd = sb("rstd", [128, 512])
    NWT = 6
    wt = [sb("wt%d" % i, [128, 8, 128], BF16) for i in range(NWT)]
    wrr = [0]
    pa = [ps("pa%d" % i, [128, 512]) for i in range(2)]
    parr = [0]
    pn = ps("pn", [128, 512]); pS = ps("pS", [128, 512]); pC = ps("pC", [128, 512]); pD = ps("pD", [128, 512]); pN = ps("pN", [128, 512])
    pT = ps("pT", [128, 1024], BF16)
    lnc = sb("lnc", [128, 8]); ln2c = sb("ln2c", [128, 8]); lnfc = sb("lnfc", [128, 8])
    tokm = sb("tokm", [128, 512]); junk = sb("junk", [128, 128]); st1 = sb("st1", [128, 8])
    sinkc = sb("sinkc", [128, 8]); cw = sb("cw", [128, 60, 4])
    dtb_t = sb("dtb_t", [128, 4]); negA = sb("negA", [128, 4]); nw_t = sb("nw_t", [128, 128])
    wab = sb("wab", [128, 8, 8], BF16); wz = sb("wz", [128, 8, 512], BF16)
    Sst = [sb("Sst%d" % i, [128, 128]) for i in range(4)]
    xh = [sb("xh%d" % i, [128, 8, HALO]) for i in range(2)]
    xm2 = [sb("xm2_%d" % i, [128, 8, 2]) for i in range(2)]

    def wload(W2, k0, nk, c0, ncol, dup=False):
        t = wt[wrr[0] % NWT]; wrr[0] += 1
        if not dup:
            src = W2[k0 * 128:(k0 + nk) * 128, c0:c0 + ncol].rearrange("(k p) n -> p k n", p=128)
            kb.dma(t[:, 0:nk, 0:ncol], src, q='pool')
            return t[:, 0:nk, 0:ncol]
        t2 = wt[wrr[0] % NWT]; wrr[0] += 1
        t3 = wt[wrr[0] % NWT]; wrr[0] += 1
        src = W2[k0 * 128:(k0 + nk) * 128, c0:c0 + 128].rearrange("(k p) n -> p k n", p=128)
        kb.dma(t[:, 0:nk, 0:128], src, q='pool')
        for hh, td in enumerate((t2, t3)):
            kb.cp(td[:, 0:nk, 0:64], t[:, 0:nk, hh * 64:(hh + 1) * 64])
            if KVAR == 1:
                kb.cp(td[:, 0:nk, 64:128], t[:, 0:nk, hh * 64:(hh + 1) * 64])
            else:
                kb.act(td[:, 0:nk, 64:128], t[:, 0:nk, hh * 64:(hh + 1) * 64], AF.Copy)
        return t2[:, 0:nk, 0:128], t3[:, 0:nk, 0:128]

    def tiles(c0, c1, step=512):
        out = []
        c = c0
        while c < c1:
            out.append((c, min(step, c1 - c))); c += step
        return out

    def proj(W2, nk, c0, ncol, rhs_fn, cols, evac, w=None):
        if w is None:
            w = wload(W2, 0, nk, c0, ncol)
        M = w.shape[2]
        for (t0, n) in cols:
            p = pa[parr[0] % 2]; parr[0] += 1
            for k in range(nk):
                kb.mm(p[0:M, 0:n], w[:, k, :], rhs_fn(k, t0, n), start=(k == 0), stop=(k == nk - 1))
            evac(p[0:M, 0:n], t0, n)

    def rmsnorm(src_fn, wcol, ncols, dst_fn):
        for (t0, n) in tiles(0, ncols):
            for c in range(8):
                s = sq[c % 2]
                kb.act(s[:, 0:n], src_fn(c, t0, n), AF.Square)
                kb.mm(pn[:, 0:n], onb[:], s[:, 0:n], start=(c == 0), stop=(c == 7))
            kb.act(rstd[:, 0:n], pn[:, 0:n], AF.Sqrt, bias=epsc[:, 0:1], scale=1.0 / D)
            kb.rcp(rstd[:, 0:n], rstd[:, 0:n])
            for c in range(8):
                kb.stt(dst_fn(c, t0, n), src_fn(c, t0, n), wcol[:, c:c + 1], rstd[:, 0:n], ALU.mult, ALU.mult)

    def loadT(dst, src2d, R):
        kb.dma(tokm[0:R, 0:128], src2d)
        kb.tr(pC[:, 0:R], tokm[0:R, 0:128], idf[0:R, 0:R])
        kb.cp(dst, pC[:, 0:R])

    def conv_fm(dst, src, wc, K, n):
        kb.ts(dst, src[:, 0:n], wc[:, 0:1], None, op0=ALU.mult)
        for j in range(1, K):
            kb.stt(dst, src[:, j:j + n], wc[:, j:j + 1], dst, ALU.mult, ALU.add)

    def conv_smp(dst3, src3, wc, K):
        kb.ts(dst3, src3[:, :, 0:4], wc[:, 0:1], None, op0=ALU.mult)
        for j in range(1, K):
            kb.stt(dst3, src3[:, :, j:j + 4], wc[:, j:j + 1], dst3, ALU.mult, ALU.add)

    def v3(ap2, k):
        return ap2.rearrange("p (s j) -> p s j", j=k)

    def load_state_T(st_l, K1, feat0, dst3):
        nr = NSQ * K1
        kb.dma(tokm[0:nr, 0:128], st_l[:, feat0:feat0 + 128])
        kb.tr(pC[:, 0:nr], tokm[0:nr, 0:128], idf[0:nr, 0:nr])
        kb.cp(dst3, v3(pC[:, 0:nr], K1))

    def store_state_T(outp_l, outs_l, feat0, K1, prow, smp3):
        kb.tr(pC[0:K1, 0:128], prow, idf[:])
        kb.cp(tokm[0:K1, 0:128], pC[0:K1, 0:128])
        kb.dma(outp_l[:, feat0:feat0 + 128], tokm[0:K1, 0:128])
        nr = NSQ * K1
        kb.cp(v3(junk[:, 0:nr], K1), smp3)
        kb.tr(pD[0:nr, 0:128], junk[:, 0:nr], idf[:])
        kb.cp(tokm[0:nr, 128:256], pD[0:nr, 0:128])
        kb.dma(outs_l.rearrange("s j f -> (s j) f")[:, feat0:feat0 + 128], tokm[0:nr, 128:256])

    nt_all = SEQ // 128
    with kb.phase():
        tin = [sb("tin%d" % i, [128, D]) for i in range(2)]
        tout = [sb("tout%d" % i, [128, 8, 128]) for i in range(2)]
        for t in range(nt_all + 1):
            ti = tin[t % 2]; to = tout[t % 2]
            rows = 128 if t < nt_all else NS
            src = xp[t * 128:(t + 1) * 128, :] if t < nt_all else xsm
            kb.dma(ti[0:rows, :], src)
            for c in range(8):
                pp = pC if c % 2 == 0 else pD
                kb.tr(pp[:, 0:rows], ti[0:rows, c * 128:(c + 1) * 128], idf[0:rows, 0:rows])
                if c % 2 == 0:
                    kb.cp(to[:, c, 0:rows], pp[:, 0:rows])
                else:
                    kb.act(to[:, c, 0:rows], pp[:, 0:rows], AF.Copy)
            kb.dma(xbuf[0][:, :, t * 128:t * 128 + rows], to[:, :, 0:rows])
    loadT(lnfc[:], lnf, 8)
    if STOP <= 0:
        kb.finish(); return nc, kb

    for l in range(L):
        xin = xbuf[l % 2]; xout = xbuf[(l + 1) % 2]
        W = w_in[l]
        loadT(lnc[:], ln1[l], 8); loadT(ln2c[:], ln2[l], 8)
        kb.dma(sinkc[:], sinks[l:l + 1, :].to_broadcast([128, 8]))
        for c in range(4):
            loadT(cw[:, c, 0:3], scw[l][:, c * 128:(c + 1) * 128], 3)
        for c in range(12):
            loadT(cw[:, 4 + c, 0:4], dncw[l][:, c * 128:(c + 1) * 128], 4)
        for c in range(44):
            loadT(cw[:, 16 + c, 0:3], fcw[l][:, c * 128:(c + 1) * 128], 3)
        kb.dma(dtb_t[:], dtb[l:l + 1, :].to_broadcast([128, 4]))
        kb.dma(negA[:], alog[l:l + 1, :].to_broadcast([128, 4]))
        kb.act(negA[:], negA[:], AF.Exp)
        kb.ts(negA[:], negA[:], -1.0, None, op0=ALU.mult)
        kb.dma(nw_t[:], dnw[l:l + 1, :].to_broadcast([128, 128]))
        kb.dma(wab[:], W[:, A0:A0 + 8].rearrange("(k p) n -> p k n", p=128), q='pool')
        kb.dma(wz[:], W[:, Z0:Z0 + 512].rearrange("(k p) n -> p k n", p=128), q='pool')
        for hd in range(4):
            kb.memset(Sst[hd][:], 0.0)

        for b in range(NBLK):
            last = (b == NBLK - 1)
            ns = NS if last else 0
            xc = NPB + ns; hc = HALO + xc
            col0 = b * NPB
            ntp = NPB // 128
            with kb.phase():
                xb_ = sb("xblk", [128, 8, XC]); h = sb("h", [128, 8, HC], BF16)
                kb.dma(xb_[:, :, 0:NPB], xin[:, :, col0:col0 + NPB])
                if last:
                    kb.dma(xb_[:, :, NPB:NPB + NS], xin[:, :, SEQ:SEQ + NS])
                xhc = xh[b % 2]; xhn = xh[(b + 1) % 2]
                if b == 0:
                    kb.memset(xhc[:], 0.0)
                kb.cp(xhn[:], xb_[:, :, NPB - HALO:NPB])
                rmsnorm(lambda c, t0, n: xhc[:, c, t0:t0 + n], lnc, HALO, lambda c, t0, n: h[:, c, t0:t0 + n])
                rmsnorm(lambda c, t0, n: xb_[:, c, t0:t0 + n], lnc, xc, lambda c, t0, n: h[:, c, HALO + t0:HALO + t0 + n])
                if STOP <= 1:
                    kb.finish(); return nc, kb
                hr = lambda k, t0, n: h[:, k, t0:t0 + n]
                own_h = tiles(HALO, hc)
                own = tiles(0, xc)

                with kb.phase():
                    brT = sb("brT", [128, 4, XC], BF16); merged = sb("merged", [128, 8, XC], BF16)
                    gate = sb("gate", [128, XC], BF16); gtmp = sb("gtmp", [128, 512])

                    def merge(br, Wb):
                        for oc_ in range(8):
                            proj(W, 8, G0 + br * 1024 + oc_ * 128, 128, hr, own_h,
                                 lambda p, t0, n: kb.act(gate[:, t0 - HALO:t0 - HALO + n], p, AF.Sigmoid))

                            def ev(p, t0, n, oc_=oc_):
                                if br == 0:
                                    kb.tt(merged[:, oc_, t0:t0 + n], p, gate[:, t0:t0 + n], ALU.mult)
                                else:
                                    kb.tt(gtmp[:, 0:n], p, gate[:, t0:t0 + n], ALU.mult)
                                    kb.tt(merged[:, oc_, t0:t0 + n], merged[:, oc_, t0:t0 + n], gtmp[:, 0:n], ALU.add)
                            proj(Wb, 4, oc_ * 128, 128, lambda k, t0, n: brT[:, k, t0:t0 + n], own, ev)

                    with kb.phase():
                        qT = sb("qT", [128, 4, XC], BF16); kT2 = [sb("kT2_%d" % i, [128, HC], BF16) for i in range(2)]
                        NTL = HC // 128 + 1
                        v2 = sb("v2", [128, NTL, 2, 128], BF16)
                        sc_s = sb("sc_s", [128, 256]); sc_e = sb("sc_e", [128, 256]); sc_p = sb("sc_p", [128, 256], BF16)
                        sc_p2 = sb("sc_p2", [4, 64], BF16); pTs = sb("pTs", [128, 2, 128], BF16)
                        kcd = sb("kcd", [128, 2, 128], BF16); vcd = sb("vcd", [128, 2, 128], BF16); kcT = sb("kcT", [128, 2, 128], BF16)
                        kvo = sb("kvo", [128, 2, 128]); kvf = sb("kvf", [128, 2, 192])
                        for c in range(4):
                            proj(W, 8, Q0 + c * 128, 128, hr, own_h,
                                 lambda p, t0, n, c=c: kb.act(qT[:, c, t0 - HALO:t0 - HALO + n], p, AF.Copy, scale=0.125))
                        if STOP <= 1.05:
                            kb.finish(); return nc, kb
                        R0 = HALO + NPB - 128
                        wk2 = wload(W, 0, 8, K0, 128, dup=True) if KVAR != 4 else (wload(W, 0, 8, K0, 128), wload(W, 0, 8, K0, 128))
                        for kvh in range(2):
                            def ev_k(p, t0, n, kvh=kvh):
                                if KVAR == 5:
                                    kb.act(kT2[kvh][:, t0:t0 + n], p, AF.Copy)
                                elif KVAR == 6:
                                    kb.ts(kT2[kvh][:, t0:t0 + n], p, 1.0, None, op0=ALU.mult)
                                else:
                                    kb.cp(kT2[kvh][:, t0:t0 + n], p)
                                if last and KVAR != 2:
                                    lo = max(t0, R0); hi = t0 + n
                                    if hi > lo:
                                        kb.act(kvf[:, kvh, lo - R0:hi - R0], p[:, lo - t0:hi - t0], AF.Copy)
                            proj(W, 8, K0, 128, hr, (own_h if KVAR == 3 else tiles(0, hc)), ev_k, w=wk2[kvh])
                        if STOP <= 1.1:
                            kb.finish(); return nc, kb
                        ntile = (hc + 127) // 128
                        wv2 = wload(W, 0, 8, V0, 128, dup=True)
                        for kvh in range(2):
                            w = wv2[kvh]
                            for t in range(ntile):
                                n = min(128, hc - t * 128)
                                p = pa[parr[0] % 2]; parr[0] += 1
                                for k in range(8):
                                    kb.mm(p[0:n, 0:128], h[:, k, t * 128:t * 128 + n], w[:, k, :], start=(k == 0), stop=(k == 7))
                                kb.cp(v2[0:n, t, kvh, :], p[0:n, 0:128])
                                if last and t >= ntile - 2 and STOP > 1.25:
                                    kb.act(kvo[0:n, kvh, :], p[0:n, 0:128], AF.Copy)
                                    if t == ntile - 2:
                                        kb.dma(o_vp[l][:, kvh * 64:(kvh + 1) * 64], kvo[0:128, kvh, 0:64])
                                    else:
                                        kb.dma(o_vs[l].rearrange("s (a i) f -> s a i f", i=4)[:, 31, :, kvh * 64:(kvh + 1) * 64],
                                               kvo[0:NS, kvh, 0:64])
                        if STOP <= 1.3:
                            kb.finish(); return nc, kb
                        if last:
                            for kvh in range(2):
                                kb.tr(pC[:, 0:128], kvf[:, kvh, 0:128], idf[:])
                                kb.cp(tokm[:, 0:64], pC[:, 0:64])
                                kb.dma(o_kp[l][:, kvh * 64:(kvh + 1) * 64], tokm[:, 0:64])
                                kb.tr(pD[0:NS, 0:128], kvf[:, kvh, 128:192], idf[:])
                                kb.cp(tokm[0:NS, 64:128], pD[0:NS, 0:64])
                                kb.dma(o_ks[l].rearrange("s (a i) f -> s a i f", i=4)[:, 31, :, kvh * 64:(kvh + 1) * 64],
                                       tokm[0:NS, 64:128])
                            kb.dma(o_ks[l][:, 0:124, :], ck[l][:, 4:128, :])
                            kb.dma(o_vs[l][:, 0:124, :], cv[l][:, 4:128, :])

                        if STOP <= 1.5:
                            kb.finish(); return nc, kb
                        def softmax(nq, NK, mask_ap, sink_ap):
                            kb.tt(sc_s[0:nq, 0:NK], pS[0:nq, 0:NK], mask_ap, ALU.add)
                            kb.rmax(st1[0:nq, 0:1], sc_s[0:nq, 0:NK])
                            kb.ts(st1[0:nq, 1:2], st1[0:nq, 0:1], sink_ap, -1.0, op0=ALU.max, op1=ALU.mult)
                            kb.act(sc_e[0:nq, 0:NK], sc_s[0:nq, 0:NK], AF.Exp, bias=st1[0:nq, 1:2], scale=1.0)
                            kb.rsum(st1[0:nq, 2:3], sc_e[0:nq, 0:NK])
                            kb.act(st1[0:nq, 3:4], sink_ap, AF.Exp, bias=st1[0:nq, 1:2], scale=1.0)
                            kb.tt(st1[0:nq, 4:5], st1[0:nq, 2:3], st1[0:nq, 3:4], ALU.add)
                            kb.rcp(st1[0:nq, 5:6], st1[0:nq, 4:5])

                        for tq in range(ntp):
                            for hq in range(8):
                                c = hq // 2; pb = (hq % 2) * 64; kvh = hq // 4
                                kb.mm(pS[:, 0:256], qT[pb:pb + 64, c, tq * 128:(tq + 1) * 128],
                                      kT2[kvh][pb:pb + 64, tq * 128:tq * 128 + 256])
                                m = mp0 if (b == 0 and tq == 0) else mp
                                softmax(128, 256, m[:, :], sinkc[:, hq:hq + 1])
                                kb.act(sc_p[:, 0:256], sc_e[:, 0:256], AF.Copy, scale=st1[:, 5:6])
                                for i in range(2):
                                    kb.tr(pT[:, i * 128:(i + 1) * 128], sc_p[:, i * 128:(i + 1) * 128], idb[:])
                                    kb.act(pTs[:, i, :], pT[:, i * 128:(i + 1) * 128], AF.Copy)
                                for i in range(2):
                                    kb.mm(pC[:, 0:128], v2[:, tq + i, kvh, :], pTs[:, i, :], start=(i == 0), stop=(i == 1))
                                kb.act(brT[pb:pb + 64, c, tq * 128:(tq + 1) * 128], pC[pb:pb + 64, 0:128], AF.Copy)
                        if STOP <= 1.7:
                            kb.finish(); return nc, kb
                        if last:
                            ts_ = ntile - 1
                            for s in range(NSQ):
                                ksrc = ck[l, s].rearrange("w (h d) -> w h d", h=2)
                                vsrc = cv[l, s].rearrange("w (h d) -> w h d", h=2)
                                kb.dma(kcd[:, :, 0:64], ksrc, q='pool'); kb.dma(kcd[:, :, 64:128], ksrc, q='pool')
                                kb.dma(vcd[:, :, 0:64], vsrc, q='pool'); kb.dma(vcd[:, :, 64:128], vsrc, q='pool')
                                for kvh in range(2):
                                    kb.tr(pT[:, 512 + kvh * 128:512 + (kvh + 1) * 128], kcd[:, kvh, :], idb[:])
                                    kb.cp(kcT[:, kvh, :], pT[:, 512 + kvh * 128:512 + (kvh + 1) * 128])
                                kb.memset(sc_p2[:], 0.0)
                                sc0 = NPB + s * 4
                                for hq in range(8):
                                    c = hq // 2; pb = (hq % 2) * 64; kvh = hq // 4
                                    kb.mm(pS[0:4, 0:128], qT[pb:pb + 64, c, sc0:sc0 + 4], kcT[pb:pb + 64, kvh, :])
                                    kb.mm(pS[0:4, 128:132], qT[pb:pb + 64, c, sc0:sc0 + 4], kT2[kvh][pb:pb + 64, HALO + sc0:HALO + sc0 + 4])
                                    softmax(4, 132, msm[:, :], sinkc[0:4, hq:hq + 1])
                                    kb.ts(sc_p[0:4, 0:128], sc_e[0:4, 0:128], st1[0:4, 5:6], None, op0=ALU.mult)
                                    kb.ts(sc_p2[0:4, s * 4:s * 4 + 4], sc_e[0:4, 128:132], st1[0:4, 5:6], None, op0=ALU.mult)
                                    kb.tr(pT[:, 0:4], sc_p[0:4, 0:128], idb[0:4, 0:4])
                                    kb.cp(pTs[:, 0, 0:4], pT[:, 0:4])
                                    kb.tr(pT[0:64, 128:132], sc_p2[0:4, 0:64], idb[0:4, 0:4])
                                    kb.cp(pTs[0:64, 1, 0:4], pT[0:64, 128:132])
                                    kb.mm(pC[:, 0:4], vcd[:, kvh, :], pTs[:, 0, 0:4], start=True, stop=False)
                                    kb.mm(pC[:, 0:4], v2[0:64, ts_, kvh, :], pTs[0:64, 1, 0:4], start=False, stop=True)
                                    kb.act(brT[pb:pb + 64, c, sc0:sc0 + 4], pC[pb:pb + 64, 0:4], AF.Copy)
                    if STOP <= 2:
                        kb.finish(); return nc, kb
                    merge(0, wba[l])
                    if STOP <= 3:
                        kb.finish(); return nc, kb

                    with kb.phase():
                        t_sh = sb("t_sh", [128, 2 + XC]); t_pr = sb("t_pr", [128, 2 + XC]); t_cu = sb("t_cu", [128, XC])
                        smA = sb("smA", [128, NSQ, 6])
                        for c in range(4):
                            cols2 = tiles(HALO - 2, hc)
                            proj(W, 8, SH0 + c * 128, 128, hr, cols2,
                                 lambda p, t0, n: kb.act(t_sh[:, t0 - (HALO - 2):t0 - (HALO - 2) + n], p, AF.Copy))
                            proj(W, 8, SC0 + c * 128, 128, hr, cols2,
                                 lambda p, t0, n: kb.tt(t_pr[:, t0 - (HALO - 2):t0 - (HALO - 2) + n], p,
                                                        t_sh[:, t0 - (HALO - 2):t0 - (HALO - 2) + n], ALU.mult))
                            conv_fm(t_cu[:, 0:NPB], t_pr, cw[:, c, :], 3, NPB)
                            if last:
                                load_state_T(st_sc[l], 2, c * 128, smA[:, :, 0:2])
                                kb.cp(smA[:, :, 2:6], v3(t_pr[:, 2 + NPB:2 + NPB + NS], 4))
                                conv_smp(v3(t_cu[:, NPB:NPB + NS], 4), smA, cw[:, c, :], 3)
                                store_state_T(o_scp[l], o_scs[l], c * 128, 2, t_pr[:, NPB:NPB + 2], smA[:, :, 4:6])
                            proj(W, 8, SB0 + c * 128, 128, hr, own_h,
                                 lambda p, t0, n, c=c: kb.tt(brT[:, c, t0 - HALO:t0 - HALO + n], p, t_cu[:, t0 - HALO:t0 - HALO + n], ALU.mult))
                    if STOP <= 4:
                        kb.finish(); return nc, kb
                    merge(1, wbb[l])

                    with kb.phase():
                        rw = sb("rw", [128, 3 + XC]); t1 = sb("t1", [128, 512]); smA = sb("smA", [128, NSQ, 7])
                        qn = sb("qn", [128, XC]); kn = sb("kn", [128, XC]); vv = sb("vv", [128, XC])
                        ntl = ntp + (1 if last else 0)
                        dsc = sb("dsc", [128, ntl, 28]); egl = sb("egl", [128, ntp, 2, 4]); egls = sb("egls", [128, 4, 16])
                        Gs = sb("Gs", [64, 4, 16])
                        kbT = sb("kbT", [128, 128]); Xs = sb("Xs", [128, 128]); Ys = sb("Ys", [128, 128])
                        Bs = [sb("Bs%d" % i, [128, 128]) for i in range(2)]; Cs = [sb("Cs%d" % i, [128, 128]) for i in range(2)]
                        Rs = [sb("Rs%d" % i, [128, 128]) for i in range(2)]; Rts = [sb("Rt%d" % i, [128, 128]) for i in range(2)]
                        Kbk = sb("Kbk", [128, 128]); Kdk = sb("Kdk", [128, 128]); Xv = sb("Xv", [128, 128])
                        WT = sb("WT", [128, 128]); Us = sb("Us", [128, 128]); QKT = sb("QKT", [128, 128]); vp = sb("vp", [128, 128])
                        osb = sb("osb", [128, 128]); zs = sb("zs", [128, 128]); ocb = sb("ocb", [128, 128], BF16)
                        jk2 = sb("jk2", [128, 128]); tA = sb("tA", [128, 128]); tB = sb("tB", [128, 128])
                        DTi = sb("DTi", [128, 128]); Di = sb("Di", [128, 128]); egrow = sb("egrow", [128, 128]); qt = sb("qt", [128, 128])
                        if last:
                            S_all = sb("S_all", [128, NSQ, 128]); WTm = sb("WTm", [128, NSQ, 64]); QTm = sb("QTm", [128, NSQ, 64])
                            Ktm = sb("Ktm", [64, NSQ, 128])
                        for t in range(ntl):
                            smp_t = (t == ntp)
                            NT = 64 if smp_t else 128
                            c0 = t * 128
                            ds = dsc[0:NT, t, :]
                            ium = iu4 if smp_t else iu
                            bom = bo4 if smp_t else bo
                            for k in range(8):
                                kb.mm(pN[0:NT, 0:8], h[:, k, HALO + c0:HALO + c0 + NT], wab[:, k, :], start=(k == 0), stop=(k == 7))
                            kb.tt(ds[:, 0:4], pN[0:NT, 0:4], dtb_t[0:NT, :], ALU.add)
                            kb.act(ds[:, 0:4], ds[:, 0:4], AF.Exp)
                            kb.act(ds[:, 0:4], ds[:, 0:4], AF.Ln, bias=1.0, scale=1.0)
                            kb.tt(ds[:, 4:8], ds[:, 0:4], negA[0:NT, :], ALU.mult)
                            kb.act(ds[:, 8:12], pN[0:NT, 4:8], AF.Sigmoid)
                            kb.mm(pD[0:NT, 0:4], ium[0:NT, 0:NT], ds[:, 4:8])
                            kb.mm(pD[0:NT, 4:8], bom[0:NT, 0:NT], ds[:, 4:8])
                            kb.cp(ds[:, 12:16], pD[0:NT, 0:4])
                            kb.ts(ds[:, 16:20], pD[0:NT, 0:4], -1.0, None, op0=ALU.mult)
                            kb.act(ds[:, 20:24], ds[:, 12:16], AF.Exp)
                            kb.tt(ds[:, 20:24], ds[:, 20:24], ds[:, 8:12], ALU.mult)
                            kb.tt(ds[:, 24:28], pD[0:NT, 4:8], ds[:, 12:16], ALU.subtract)
                            kb.act(ds[:, 24:28], ds[:, 24:28], AF.Exp)
                            if not smp_t:
                                for x in range(2):
                                    kb.mm(pD[:, 8 + 4 * x:12 + 4 * x], onf[64 * x:64 * x + 64, :], dsc[64 * x:64 * x + 64, t, 4:8])
                                    kb.act(egl[:, t, x, :], pD[:, 8 + 4 * x:12 + 4 * x], AF.Exp)
                            else:
                                kb.tt(Gs[:], bi[:, :].unsqueeze(1).to_broadcast([64, 4, 16]),
                                      dsc[0:64, t, 4:8].unsqueeze(2).to_broadcast([64, 4, 16]), ALU.mult)
                                kb.mm(pD[:, 16:80], onf[0:64, :], Gs[:].rearrange("p h s -> p (h s)"))
                                kb.act(egls[:].rearrange("p h s -> p (h s)"), pD[:, 16:80], AF.Exp)

                        for hd in range(4):
                            for part, dst in enumerate((qn, kn, vv)):
                                fo = part * 512 + hd * 128
                                ci = 4 + part * 4 + hd
                                proj(W, 8, DN0 + fo, 128, hr, tiles(HALO - 3, hc),
                                     lambda p, t0, n: kb.act(rw[:, t0 - (HALO - 3):t0 - (HALO - 3) + n], p, AF.Copy))
                                conv_fm(dst[:, 0:NPB], rw, cw[:, ci, :], 4, NPB)
                                if last:
                                    load_state_T(st_dnc[l], 3, fo, smA[:, :, 0:3])
                                    kb.cp(smA[:, :, 3:7], v3(rw[:, 3 + NPB:3 + NPB + NS], 4))
                                    conv_smp(v3(dst[:, NPB:NPB + NS], 4), smA, cw[:, ci, :], 4)
                                    store_state_T(o_dncp[l], o_dncs[l], fo, 3, rw[:, NPB:NPB + 3], smA[:, :, 4:7])
                                kb.act(dst[:, 0:xc], dst[:, 0:xc], AF.Silu)
                                if part < 2:
                                    for (t0, n) in tiles(0, xc):
                                        kb.act(t1[:, 0:n], dst[:, t0:t0 + n], AF.Square)
                                        kb.mm(pn[:, 0:n], onf[:], t1[:, 0:n])
                                        kb.act(rstd[:, 0:n], pn[:, 0:n], AF.Sqrt, bias=epsc[:, 0:1], scale=1.0)
                                        kb.rcp(rstd[:, 0:n], rstd[:, 0:n])
                                        if part == 0:
                                            kb.stt(dst[:, t0:t0 + n], dst[:, t0:t0 + n], 128.0 ** -0.5, rstd[:, 0:n], ALU.mult, ALU.mult)
                                        else:
                                            kb.tt(dst[:, t0:t0 + n], dst[:, t0:t0 + n], rstd[:, 0:n], ALU.mult)
                            for t in range(ntl):
                                smp_t = (t == ntp)
                                NT = 64 if smp_t else 128
                                c0 = t * 128
                                cs = slice(c0, c0 + NT)
                                sum_, slm, mniu_, mnil_ = (su4, sl4, mniu4, mnil4) if smp_t else (su, sl, mniu, mnil)
                                levels = 1 if smp_t else 5
                                rr = slice(0, NT)
                                col = lambda j: dsc[0:NT, t, j + hd:j + hd + 1]
                                kb.mm(pN[:, 0:NT], col(8).to_broadcast([NT, 128]), idf[0:NT, 0:NT])
                                kb.tt(kbT[:, 0:NT], kn[:, cs], pN[:, 0:NT], ALU.mult)
                                kb.mm(pN[:, 128:128 + NT], col(12).to_broadcast([NT, 128]), idf[0:NT, 0:NT])
                                kb.tt(tA[rr, 0:NT], pN[0:NT, 128:128 + NT], mniu_[0:NT, 0:NT], ALU.add)
                                kb.act(DTi[rr, 0:NT], tA[rr, 0:NT], AF.Exp, bias=col(16), scale=1.0)
                                kb.stt(tB[rr, 0:NT], pN[0:NT, 128:128 + NT], -1.0, mnil_[0:NT, 0:NT], ALU.mult, ALU.add)
                                kb.act(Di[rr, 0:NT], tB[rr, 0:NT], AF.Exp, bias=col(12), scale=1.0)
                                kb.act(egrow[:, 0:NT], pN[:, 128:128 + NT], AF.Exp)
                                kb.tt(qt[:, 0:NT], qn[:, cs], egrow[:, 0:NT], ALU.mult)
                                kb.mm(pS[0:NT, 0:NT], kn[:, cs], kbT[:, 0:NT])
                                kb.tt(Xs[rr, 0:NT], pS[0:NT, 0:NT], sum_[0:NT, 0:NT], ALU.mult)
                                kb.tt(Xs[rr, 0:NT], Xs[rr, 0:NT], DTi[rr, 0:NT], ALU.mult)
                                kb.mm(pS[0:NT, 128:128 + NT], kbT[:, 0:NT], kn[:, cs])
                                kb.tt(Ys[rr, 0:NT], pS[0:NT, 128:128 + NT], slm[0:NT, 0:NT], ALU.mult)
                                kb.tt(Ys[rr, 0:NT], Ys[rr, 0:NT], Di[rr, 0:NT], ALU.mult)
                                kb.tt(Rs[0][rr, 0:NT], idf[0:NT, 0:NT], Xs[rr, 0:NT], ALU.subtract)
                                kb.tt(Rts[0][rr, 0:NT], idf[0:NT, 0:NT], Ys[rr, 0:NT], ALU.subtract)
                                Bc, Cc, r = Xs, Ys, 0
                                for nlv in range(levels):
                                    lastlv = (nlv == levels - 1)
                                    Bn = Bs[nlv % 2]; Cn = Cs[nlv % 2]
                                    kb.mm(pN[0:NT, 0:NT], Cc[rr, 0:NT], Bc[rr, 0:NT])
                                    kb.act(Bn[rr, 0:NT], pN[0:NT, 0:NT], AF.Copy)
                                    if not lastlv:
                                        kb.mm(pN[0:NT, 128:128 + NT], Bc[rr, 0:NT], Cc[rr, 0:NT])
                                        kb.act(Cn[rr, 0:NT], pN[0:NT, 128:128 + NT], AF.Copy)
                                    kb.mm(pS[0:NT, 0:NT], Rts[r][rr, 0:NT], Bn[rr, 0:NT])
                                    kb.tt(Rs[1 - r][rr, 0:NT], Rs[r][rr, 0:NT], pS[0:NT, 0:NT], ALU.add)
                                    if not lastlv:
                                        kb.mm(pS[0:NT, 128:128 + NT], Bn[rr, 0:NT], Rts[r][rr, 0:NT])
                                        kb.tt(Rts[1 - r][rr, 0:NT], Rts[r][rr, 0:NT], pS[0:NT, 128:128 + NT], ALU.add)
                                    Bc, Cc, r = Bn, Cn, 1 - r
                                TT = Rs[r]
                                kb.tr(pC[0:NT, 0:128], kn[:, cs], idf[:])
                                kb.ts(Kbk[rr, :], pC[0:NT, 0:128], col(20), None, op0=ALU.mult)
                                kb.act(Kdk[rr, :], pC[0:NT, 0:128], AF.Copy, scale=col(24))
                                kb.tr(pD[0:NT, 0:128], vv[:, cs], idf[:])
                                kb.act(Xv[rr, :], pD[0:NT, 0:128], AF.Copy, scale=col(8))
                                kb.mm(pC[:, 128:128 + NT], Kbk[rr, :], TT[rr, 0:NT])
                                kb.act(WT[:, 0:NT], pC[:, 128:128 + NT], AF.Copy)
                                kb.mm(pD[0:NT, 128:256], TT[rr, 0:NT], Xv[rr, :])
                                kb.act(Us[rr, :], pD[0:NT, 128:256], AF.Copy)
                                kb.mm(pS[0:NT, 256:256 + NT], kn[:, cs], qn[:, cs])
                                kb.tt(QKT[rr, 0:NT], pS[0:NT, 256:256 + NT], DTi[rr, 0:NT], ALU.mult)
                                pz = pa[parr[0] % 2]; parr[0] += 1
                                for k in range(8):
                                    kb.mm(pz[0:NT, 0:128], h[:, k, HALO + c0:HALO + c0 + NT], wz[:, k, hd * 128:(hd + 1) * 128],
                                          start=(k == 0), stop=(k == 7))
                                kb.act(zs[rr, :], pz[0:NT, 0:128], AF.Silu)
                                if not smp_t:
                                    S = Sst[hd]
                                    for x in range(2):
                                        r_ = slice(64 * x, 64 * x + 64)
                                        kb.mm(pC[:, 0:128], WT[:, 0:128], S[:])
                                        kb.tt(vp[r_, :], Us[r_, :], pC[r_, 0:128], ALU.subtract)
                                        kb.mm(pD[:, 0:128], qt[:, 0:128], S[:], start=True, stop=False)
                                        kb.mm(pD[:, 0:128], QKT[r_, 0:128], vp[r_, :], start=False, stop=True)
                                        kb.act(osb[r_, :], pD[r_, 0:128], AF.Copy)
                                        kb.mm(pC[:, 256:384], Kdk[r_, :], vp[r_, :])
                                        kb.stt(S[:], S[:], egl[:, t, x, hd:hd + 1], pC[:, 256:384], ALU.mult, ALU.add)
                                else:
                                    kb.dma(S_all[:], st_dn[l, :, hd].rearrange("s d e -> d s e"))
                                    kb.tt(WTm[:], WT[:, 0:64].unsqueeze(1).to_broadcast([128, NSQ, 64]), bm[:], ALU.mult)
                                    kb.tt(QTm[:], qt[:, 0:64].unsqueeze(1).to_broadcast([128, NSQ, 64]), bm[:], ALU.mult)
                                    kb.tt(Ktm[:], Kdk[0:64, :].unsqueeze(1).to_broadcast([64, NSQ, 128]),
                                          bi[:, :].unsqueeze(2).to_broadcast([64, NSQ, 128]), ALU.mult)
                                    for s_ in range(NSQ):
                                        kb.mm(pC[0:64, 0:128], WTm[:, s_, :], S_all[:, s_, :], start=(s_ == 0), stop=(s_ == NSQ - 1))
                                    kb.tt(vp[0:64, :], Us[0:64, :], pC[0:64, 0:128], ALU.subtract)
                                    for s_ in range(NSQ):
                                        kb.mm(pD[0:64, 0:128], QTm[:, s_, :], S_all[:, s_, :], start=(s_ == 0), stop=False)
                                    kb.mm(pD[0:64, 0:128], QKT[0:64, 0:64], vp[0:64, :], start=False, stop=True)
                                    kb.act(osb[0:64, :], pD[0:64, 0:128], AF.Copy)
                                    for s_ in range(NSQ):
                                        kb.mm(pC[:, 256:384], Ktm[:, s_, :], vp[0:64, :])
                                        kb.stt(S_all[:, s_, :], S_all[:, s_, :], egls[:, hd, s_:s_ + 1], pC[:, 256:384], ALU.mult, ALU.add)
                                    kb.dma(o_dns[l, :, hd].rearrange("s d e -> d s e"), S_all[:])
                                kb.act(jk2[rr, :], osb[rr, :], AF.Square)
                                kb.rsum(st1[rr, 0:1], jk2[rr, :])
                                kb.act(st1[rr, 1:2], st1[rr, 0:1], AF.Sqrt, bias=epsc[rr, 0:1], scale=1.0 / 128)
                                kb.rcp(st1[rr, 2:3], st1[rr, 1:2])
                                kb.stt(osb[rr, :], osb[rr, :], st1[rr, 2:3], nw_t[rr, :], ALU.mult, ALU.mult)
                                kb.tt(ocb[rr, :], osb[rr, :], zs[rr, :], ALU.mult)
                                kb.tr(pT[:, 0:NT], ocb[rr, :], idb[0:NT, 0:NT])
                                kb.cp(brT[:, hd, cs], pT[:, 0:NT])
                        if last:
                            for hd in range(4):
                                kb.dma(o_dnp[l, hd], Sst[hd][:])
                    if STOP <= 5:
                        kb.finish(); return nc, kb
                    merge(2, wbc[l])

                    for oc_ in range(8):
                        proj(wo[l], 8, oc_ * 128, 128, lambda k, t0, n: merged[:, k, t0:t0 + n], own,
                             lambda p, t0, n, oc_=oc_: kb.tt(xb_[:, oc_, t0:t0 + n], xb_[:, oc_, t0:t0 + n], p, ALU.add))

                if STOP <= 6:
                    kb.finish(); return nc, kb
                xmc = xm2[b % 2]; xmn = xm2[(b + 1) % 2]
                if b == 0:
                    kb.memset(xmc[:], 0.0)
                kb.cp(xmn[:], xb_[:, :, NPB - 2:NPB])
                rmsnorm(lambda c, t0, n: xmc[:, c, t0:t0 + n], ln2c, 2, lambda c, t0, n: h[:, c, HALO - 2 + t0:HALO - 2 + t0 + n])
                rmsnorm(lambda c, t0, n: xb_[:, c, t0:t0 + n], ln2c, xc, lambda c, t0, n: h[:, c, HALO + t0:HALO + t0 + n])
                with kb.phase():
                    act_ = sb("act", [128, 22, XC], BF16)
                    rg = sb("rg", [128, 2 + XC]); rv = sb("rv", [128, 2 + XC]); cg = sb("cg", [128, XC]); cvv = sb("cvv", [128, XC])
                    smG = sb("smG", [128, NSQ, 6]); smV = sb("smV", [128, NSQ, 6])
                    cols2 = tiles(HALO - 2, hc)
                    for c in range(22):
                        for (f0, raw, cd, smX, ci) in ((c * 128, rg, cg, smG, 16 + c), (DFF + c * 128, rv, cvv, smV, 16 + 22 + c)):
                            proj(fup[l], 8, f0, 128, hr, cols2,
                                 lambda p, t0, n, raw=raw: kb.act(raw[:, t0 - (HALO - 2):t0 - (HALO - 2) + n], p, AF.Copy))
                            conv_fm(cd[:, 0:NPB], raw, cw[:, ci, :], 3, NPB)
                            if last:
                                load_state_T(st_ff[l], 2, f0, smX[:, :, 0:2])
                                kb.cp(smX[:, :, 2:6], v3(raw[:, 2 + NPB:2 + NPB + NS], 4))
                                conv_smp(v3(cd[:, NPB:NPB + NS], 4), smX, cw[:, ci, :], 3)
                                store_state_T(o_ffp[l], o_ffs[l], f0, 2, raw[:, NPB:NPB + 2], smX[:, :, 4:6])
                        kb.act(cg[:, 0:xc], cg[:, 0:xc], AF.Silu)
                        kb.tt(act_[:, c, 0:xc], cg[:, 0:xc], cvv[:, 0:xc], ALU.mult)
                    for oc_ in range(8):
                        ws = [wload(fdn[l], k0, nk, oc_ * 128, 128) for (k0, nk) in ((0, 8), (8, 8), (16, 6))]
                        for (t0, n) in own:
                            p = pa[parr[0] % 2]; parr[0] += 1
                            for k in range(22):
                                kb.mm(p[:, 0:n], ws[k // 8][:, k % 8, :], act_[:, k, t0:t0 + n], start=(k == 0), stop=(k == 21))
                            kb.tt(xb_[:, oc_, t0:t0 + n], xb_[:, oc_, t0:t0 + n], p[:, 0:n], ALU.add)
                kb.dma(xout[:, :, col0:col0 + NPB], xb_[:, :, 0:NPB])
                if last:
                    kb.dma(xout[:, :, SEQ:SEQ + NS], xb_[:, :, NPB:NPB + NS])

    if STOP <= 7:
        kb.finish(); return nc, kb
    xfin = xbuf[L % 2]
    with kb.phase():
        xf = [sb("xf%d" % i, [128, 8, 512]) for i in range(2)]
        yf = sb("yf", [128, 8, 512]); yt = [sb("yt%d" % i, [128, D]) for i in range(2)]
        for bi_, (t0, n) in enumerate(tiles(0, NTOT)):
            xx = xf[bi_ % 2]
            kb.dma(xx[:, :, 0:n], xfin[:, :, t0:t0 + n])
            rmsnorm(lambda c, a, m: xx[:, c, a:a + m], lnfc, n, lambda c, a, m: yf[:, c, a:a + m])
            for j in range((n + 127) // 128):
                rows = min(128, n - j * 128)
                y_ = yt[j % 2]
                for c in range(8):
                    pp = pC if c % 2 == 0 else pD
                    kb.tr(pp[0:rows, 0:128], yf[:, c, j * 128:j * 128 + rows], idf[:])
                    if c % 2 == 0:
                        kb.cp(y_[0:rows, c * 128:(c + 1) * 128], pp[0:rows, 0:128])
                    else:
                        kb.act(y_[0:rows, c * 128:(c + 1) * 128], pp[0:rows, 0:128], AF.Copy)
                g0 = t0 + j * 128
                if g0 < SEQ:
                    kb.dma(yp[g0:g0 + rows, :], y_[0:rows, :])
                else:
                    kb.dma(ys[g0 - SEQ:g0 - SEQ + rows, :], y_[0:rows, :])
    kb.finish()
    return nc, kb


def _consts():
    i = np.arange(128)
    c = {}
    c["c_id"] = np.eye(128, dtype=np.float32)
    q = i[:, None]; j = i[None, :]
    prev = np.where(j >= q, 0.0, NEG); cur = np.where(j <= q, 0.0, NEG)
    c["c_mp"] = np.concatenate([prev, cur], 1).astype(np.float32)
    c["c_mp0"] = np.concatenate([np.full((128, 128), NEG), cur], 1).astype(np.float32)
    qi = np.arange(4)[:, None]
    c["c_ms"] = np.concatenate([np.where(np.arange(128)[None, :] >= qi, 0.0, NEG), np.where(np.arange(4)[None, :] <= qi, 0.0, NEG)], 1).astype(np.float32)
    for nm, n, blk in (("", 128, 64), ("4", 64, 4)):
        a = np.arange(n)
        same = (a[:, None] // blk) == (a[None, :] // blk)
        c["c_su" + nm] = (same & (a[:, None] < a[None, :])).astype(np.float32)
        c["c_sl" + nm] = (same & (a[:, None] > a[None, :])).astype(np.float32)
        c["c_iu" + nm] = (same & (a[:, None] <= a[None, :])).astype(np.float32)
        c["c_bo" + nm] = same.astype(np.float32)
        c["c_mniu" + nm] = np.where(same & (a[:, None] <= a[None, :]), 0.0, NEG).astype(np.float32)
        c["c_mnil" + nm] = np.where(same & (a[:, None] >= a[None, :]), 0.0, NEG).astype(np.float32)
    t = np.arange(64)
    bi = ((t[:, None] // 4) == np.arange(16)[None, :]).astype(np.float32)
    c["c_bi"] = bi
    c["c_bm"] = np.broadcast_to(bi.T.reshape(1, 16 * 64), (128, 16 * 64)).astype(np.float32).copy()
    return c


_CACHE = {}


def kernel(x_prompt, x_sample, cache_swa_k, cache_swa_v, state_sc_conv, state_dn_conv, state_dn, state_ffn_conv,
           ln1, w_in, attn_sinks, sc_conv_w, dn_conv_w, dn_A_log, dn_dt_bias, dn_norm_w,
           w_br_a, w_br_b, w_br_c, w_o, ln2, ffn_up, ffn_conv_w, ffn_down, ln_f):
    f = lambda a: np.ascontiguousarray(np.asarray(a, dtype=np.float32))
    x_prompt = f(x_prompt); x_sample = f(x_sample)
    Bp, SEQ, _ = x_prompt.shape
    L = int(np.asarray(ln1).shape[0])
    NSEQS = x_sample.shape[0]
    key = (SEQ, L)
    if key not in _CACHE:
        _CACHE[key] = build(SEQ, L)[0]
    nc = _CACHE[key]
    consts = _consts()
    shared = dict(ln1=f(ln1).reshape(L, 8, 128), w_in=f(w_in), sinks=f(attn_sinks), scw=f(sc_conv_w), dncw=f(dn_conv_w),
                  alog=f(dn_A_log), dtb=f(dn_dt_bias), dnw=f(dn_norm_w), wba=f(w_br_a), wbb=f(w_br_b), wbc=f(w_br_c), wo=f(w_o),
                  ln2=f(ln2).reshape(L, 8, 128), fup=f(ffn_up), fcw=f(ffn_conv_w), fdn=f(ffn_down), lnf=f(ln_f).reshape(8, 128))
    shared.update(consts)
    ck = f(cache_swa_k).reshape(L, NSEQS, 128, 128); cv = f(cache_swa_v).reshape(L, NSEQS, 128, 128)
    ssc = f(state_sc_conv); sdc = f(state_dn_conv); sdn = f(state_dn); sff = f(state_ffn_conv)
    in_maps = []
    for c in range(NCORE):
        sl_ = slice(c * NSQ, (c + 1) * NSQ)
        m = dict(shared)
        m["xp"] = x_prompt[c % Bp]
        m["xsm"] = np.ascontiguousarray(x_sample[sl_].reshape(NS, D))
        m["ck"] = np.ascontiguousarray(ck[:, sl_]); m["cv"] = np.ascontiguousarray(cv[:, sl_])
        m["st_sc"] = np.ascontiguousarray(ssc[:, sl_].reshape(L, NSQ * 2, 512))
        m["st_dnc"] = np.ascontiguousarray(sdc[:, sl_].reshape(L, NSQ * 3, 1536))
        m["st_dn"] = np.ascontiguousarray(sdn[:, sl_])
        m["st_ff"] = np.ascontiguousarray(sff[:, sl_].reshape(L, NSQ * 2, 2 * DFF))
        in_maps.append(m)
    res = run_bass_kernel_spmd(nc, in_maps, core_ids=list(range(NCORE)))
    R = res.results
    cat = lambda k, ax=1: np.concatenate([np.asarray(R[c][k]) for c in range(NCORE)], axis=ax)
    stk = lambda k: np.stack([np.asarray(R[c][k]) for c in range(Bp)], axis=1)
    y_prompt = np.stack([np.asarray(R[c]["yp"]) for c in range(Bp)], 0)
    y_sample = np.concatenate([np.asarray(R[c]["ys"]).reshape(NSQ, 4, D) for c in range(NCORE)], 0)
    kp = stk("o_kp").reshape(L, Bp, 128, 2, 64); vp_ = stk("o_vp").reshape(L, Bp, 128, 2, 64)
    scp = stk("o_scp"); dncp = stk("o_dncp"); dnp = stk("o_dnp"); ffp = stk("o_ffp")
    ks = cat("o_ks").reshape(L, NSEQS, 128, 2, 64); vs = cat("o_vs").reshape(L, NSEQS, 128, 2, 64)
    scs = cat("o_scs"); dncs = cat("o_dncs"); dns = cat("o_dns"); ffs = cat("o_ffs")
    outs = (y_prompt, y_sample, kp, vp_, scp, dncp, dnp, ffp, ks, vs, scs, dncs, dns, ffs)
    return tuple(np.ascontiguousarray(o, dtype=np.float32) for o in outs)
```

```python
import numpy as np
from contextlib import ExitStack
import concourse.bass as bass
import concourse.mybir as mybir
from concourse.bass_utils import run_bass_kernel_spmd

F32 = mybir.dt.float32
BF16 = mybir.dt.bfloat16
AF = mybir.ActivationFunctionType
ALU = mybir.AluOpType
AX = mybir.AxisListType

D = 1024
NCORE = 8
NSQ = 16
NS = 64
NPB = 1024
HALO = 128
DFF = 2816
INC = 7432
Q0, K0, V0, SH0, SB0, SC0, DN0, Z0, A0, B0, G0 = 0, 512, 640, 768, 1280, 1792, 2304, 3840, 4352, 4356, 4360
NEG = -30000.0


class KB:
    def __init__(self):
        self.nc = bass.Bass("TRN2", target_bir_lowering=False)
        self.es = ExitStack()
        nc = self.nc
        self.E = dict(pe=nc.tensor, act=nc.scalar, dve=nc.vector, pool=nc.gpsimd, sp=nc.sync)
        self.semobj = {}
        self.ccnt = {}
        for e in ('pe', 'act', 'dve', 'pool'):
            self.semobj['c_' + e] = self.es.enter_context(nc.semaphore('c_' + e))
            self.ccnt[e] = 0
        self.dq = {}
        for q in ('sp', 'pool'):
            ids = []
            for i in range(16):
                sid = 'd_%s%d' % (q, i)
                self.semobj[sid] = self.es.enter_context(nc.semaphore(sid))
                ids.append(sid)
            self.dq[q] = [ids, 0]
        self.dcnt = {}
        self.waited = {e: {} for e in self.E}
        self.lastw = {}
        self.readers = {}
        self.ninst = 0
        self.cur = self.es
        self.pend = {}
        self.psum_names = set()
        self.seen = set()
        self.pnames = None
        self.uid = 0

    def sb(self, name, shape, dt=F32):
        if self.pnames is not None:
            self.uid += 1
            name = "%s_u%d" % (name, self.uid)
            self.pnames[-1].append(name)
        return self.cur.enter_context(self.nc.sbuf_tensor(name, shape, dt))

    def phase(self):
        kb = self

        class _P:
            def __enter__(s):
                s.st = ExitStack(); s.prev = kb.cur; kb.cur = s.st
                if kb.pnames is None:
                    kb.pnames = []
                kb.pnames.append([])
                return s

            def __exit__(s, *a):
                names = kb.pnames.pop()
                if not kb.pnames:
                    kb.pnames = None
                for nm in names:
                    t = kb.lastw.pop(nm, None)
                    if t:
                        kb.pend[t[0]] = max(kb.pend.get(t[0], 0), t[1])
                    for sid, v in kb.readers.pop(nm, {}).items():
                        kb.pend[sid] = max(kb.pend.get(sid, 0), v)
                kb.cur = s.prev
                s.st.close()
                return False
        return _P()

    def ps(self, name, shape, dt=F32):
        self.psum_names.add(name)
        return self.es.enter_context(self.nc.psum_tensor(name, shape, dt))

    @staticmethod
    def _keys(aps):
        ks = []
        for a in aps:
            if a is None or isinstance(a, (int, float)):
                continue
            ks.append(a.tensor.name)
        return ks

    def emit(self, e, fn, reads, writes, dma=False):
        reads = self._keys(reads)
        writes = self._keys(writes)
        need = {}
        for k in reads:
            t = self.lastw.get(k)
            if t:
                need[t[0]] = max(need.get(t[0], 0), t[1])
        for k in writes:
            t = self.lastw.get(k)
            if t:
                need[t[0]] = max(need.get(t[0], 0), t[1])
            for sid, v in self.readers.get(k, {}).items():
                need[sid] = max(need.get(sid, 0), v)
        for k in reads:
            if k in self.psum_names:
                for sid, v in self.readers.get(k, {}).items():
                    if sid != 'c_' + e:
                        need[sid] = max(need.get(sid, 0), v)
        for k in reads + writes:
            if k not in self.seen:
                self.seen.add(k)
                for sid, v in self.pend.items():
                    need[sid] = max(need.get(sid, 0), v)
        for sid, v in need.items():
            if e == 'pe' and sid == 'c_pe':
                continue
            if self.waited[e].get(sid, 0) < v:
                self.E[e].wait_ge(self.semobj[sid], v)
                self.waited[e][sid] = v
        if dma:
            ids, rr = self.dq[e]
            sid = ids[rr % len(ids)]
            self.dq[e][1] = rr + 1
            if self.dcnt.get(sid, 0) > self.waited[e].get(sid, 0):
                self.E[e].wait_ge(self.semobj[sid], self.dcnt[sid])
                self.waited[e][sid] = self.dcnt[sid]
        ins = fn()
        self.ninst += 1
        if dma:
            self.dcnt[sid] = self.dcnt.get(sid, 0) + 16
            ins.then_inc(self.semobj[sid], 16)
            tok = (sid, self.dcnt[sid])
        else:
            self.ccnt[e] += 1
            ins.then_inc(self.semobj['c_' + e], 1)
            tok = ('c_' + e, self.ccnt[e])
        for k in reads:
            r = self.readers.setdefault(k, {})
            r[tok[0]] = max(r.get(tok[0], 0), tok[1])
        for k in writes:
            self.lastw[k] = tok
            self.readers[k] = {}
        return ins

    def mm(self, out, lhsT, rhs, start=True, stop=True):
        return self.emit('pe', lambda: self.nc.tensor.matmul(out, lhsT, rhs, start=start, stop=stop), [lhsT, rhs], [out])

    def tr(self, out, in_, ident):
        return self.emit('pe', lambda: self.nc.tensor.transpose(out, in_, ident), [in_, ident], [out])

    def act(self, out, in_, func, bias=None, scale=None, accum=None):
        kw = {}
        if bias is not None:
            kw['bias'] = bias
        if scale is not None:
            kw['scale'] = scale
        if accum is not None:
            kw['accum_out'] = accum
        return self.emit('act', lambda: self.nc.scalar.activation(out=out, in_=in_, func=func, **kw),
                         [in_, bias, scale], [out, accum])

    def tt(self, out, a, b, op, eng='dve'):
        return self.emit(eng, lambda: self.E[eng].tensor_tensor(out=out, in0=a, in1=b, op=op), [a, b], [out])

    def ts(self, out, a, s1, s2=None, op0=ALU.mult, op1=None, eng='dve'):
        kw = {}
        if op1 is not None:
            kw['op1'] = op1
        return self.emit(eng, lambda: self.E[eng].tensor_scalar(out=out, in0=a, scalar1=s1, scalar2=s2, op0=op0, **kw),
                         [a, s1, s2], [out])

    def stt(self, out, a, scalar, b, op0, op1, eng='dve'):
        return self.emit(eng, lambda: self.E[eng].scalar_tensor_tensor(out=out, in0=a, scalar=scalar, in1=b, op0=op0, op1=op1),
                         [a, scalar, b], [out])

    def cp(self, out, in_, eng='dve'):
        return self.emit(eng, lambda: self.E[eng].tensor_copy(out=out, in_=in_), [in_], [out])

    def rmax(self, out, in_):
        return self.emit('dve', lambda: self.nc.vector.reduce_max(out=out, in_=in_, axis=AX.X), [in_], [out])

    def rsum(self, out, in_):
        return self.emit('dve', lambda: self.nc.vector.reduce_sum(out=out, in_=in_, axis=AX.X), [in_], [out])

    def rcp(self, out, in_):
        return self.emit('dve', lambda: self.nc.vector.reciprocal(out=out, in_=in_), [in_], [out])

    def memset(self, ap, val, eng='dve'):
        return self.emit(eng, lambda: self.E[eng].memset(ap, val), [], [ap])

    def dma(self, out, in_, q='sp'):
        return self.emit(q, lambda: self.E[q].dma_start(out=out, in_=in_), [in_], [out], dma=True)

    def finish(self):
        for sid, v in list(self.dcnt.items()):
            self.nc.sync.wait_ge(self.semobj[sid], v)
        for e, v in self.ccnt.items():
            if v:
                self.nc.sync.wait_ge(self.semobj['c_' + e], v)


def build(SEQ, L):
    import os
    STOP = float(os.environ.get('KSTOP', '99'))
    KVAR = int(os.environ.get('KVAR', '0'))
    NBLK = SEQ // NPB
    kb = KB()
    nc = kb.nc

    def din(name, shape, dt=F32):
        return nc.dram_tensor(name, list(shape), dt, kind="ExternalInput").ap()

    def dout(name, shape):
        return nc.dram_tensor(name, list(shape), F32, kind="ExternalOutput").ap()

    xp = din("xp", [SEQ, D]); xsm = din("xsm", [NS, D])
    ck = din("ck", [L, NSQ, 128, 128]); cv = din("cv", [L, NSQ, 128, 128])
    st_sc = din("st_sc", [L, NSQ * 2, 512]); st_dnc = din("st_dnc", [L, NSQ * 3, 1536])
    st_dn = din("st_dn", [L, NSQ, 4, 128, 128]); st_ff = din("st_ff", [L, NSQ * 2, 2 * DFF])
    ln1 = din("ln1", [L, 8, 128]); w_in = din("w_in", [L, D, INC]); sinks = din("sinks", [L, 8])
    scw = din("scw", [L, 3, 512]); dncw = din("dncw", [L, 4, 1536]); alog = din("alog", [L, 4]); dtb = din("dtb", [L, 4])
    dnw = din("dnw", [L, 128]); wba = din("wba", [L, 512, D]); wbb = din("wbb", [L, 512, D]); wbc = din("wbc", [L, 512, D])
    wo = din("wo", [L, D, D]); ln2 = din("ln2", [L, 8, 128]); fup = din("fup", [L, D, 2 * DFF]); fcw = din("fcw", [L, 3, 2 * DFF])
    fdn = din("fdn", [L, DFF, D]); lnf = din("lnf", [8, 128])
    c_id = din("c_id", [128, 128]); c_mp = din("c_mp", [128, 256]); c_mp0 = din("c_mp0", [128, 256]); c_ms = din("c_ms", [4, 132])
    c_su = din("c_su", [128, 128]); c_sl = din("c_sl", [128, 128]); c_iu = din("c_iu", [128, 128])
    c_su4 = din("c_su4", [64, 64]); c_sl4 = din("c_sl4", [64, 64]); c_iu4 = din("c_iu4", [64, 64])
    c_bi = din("c_bi", [64, 16]); c_bm = din("c_bm", [128, 16 * 64])
    c_bo = din("c_bo", [128, 128]); c_bo4 = din("c_bo4", [64, 64])
    c_mniu = din("c_mniu", [128, 128]); c_mnil = din("c_mnil", [128, 128]); c_mniu4 = din("c_mniu4", [64, 64]); c_mnil4 = din("c_mnil4", [64, 64])

    yp = dout("yp", [SEQ, D]); ys = dout("ys", [NS, D])
    o_kp = dout("o_kp", [L, 128, 128]); o_vp = dout("o_vp", [L, 128, 128])
    o_scp = dout("o_scp", [L, 2, 512]); o_dncp = dout("o_dncp", [L, 3, 1536]); o_dnp = dout("o_dnp", [L, 4, 128, 128])
    o_ffp = dout("o_ffp", [L, 2, 2 * DFF])
    o_ks = dout("o_ks", [L, NSQ, 128, 128]); o_vs = dout("o_vs", [L, NSQ, 128, 128])
    o_scs = dout("o_scs", [L, NSQ, 2, 512]); o_dncs = dout("o_dncs", [L, NSQ, 3, 1536]); o_dns = dout("o_dns", [L, NSQ, 4, 128, 128])
    o_ffs = dout("o_ffs", [L, NSQ, 2, 2 * DFF])

    NTOT = SEQ + NS
    xbuf = [nc.dram_tensor("xbuf%d" % i, [128, 8, NTOT], F32).ap() for i in range(2)]

    sb, ps = kb.sb, kb.ps
    HC = HALO + NPB + NS
    XC = NPB + NS
    idf = sb("idf", [128, 128]); idb = sb("idb", [128, 128], BF16); onb = sb("onb", [128, 128], BF16); onf = sb("onf", [128, 128])
    mp = sb("mp", [128, 256]); mp0 = sb("mp0", [128, 256]); msm = sb("msm", [4, 132])
    su = sb("su", [128, 128]); sl = sb("sl", [128, 128]); iu = sb("iu", [128, 128])
    su4 = sb("su4", [64, 64]); sl4 = sb("sl4", [64, 64]); iu4 = sb("iu4", [64, 64]); bi = sb("bi", [64, 16]); bm = sb("bm", [128, 16, 64])
    bo = sb("bo", [128, 128]); bo4 = sb("bo4", [64, 64]); mniu = sb("mniu", [128, 128]); mnil = sb("mnil", [128, 128])
    mniu4 = sb("mniu4", [64, 64]); mnil4 = sb("mnil4", [64, 64])
    for t_, c_ in ((bo, c_bo), (bo4, c_bo4), (mniu, c_mniu), (mnil, c_mnil), (mniu4, c_mniu4), (mnil4, c_mnil4), (idf, c_id), (mp, c_mp), (mp0, c_mp0), (msm, c_ms), (su, c_su), (sl, c_sl), (iu, c_iu), (su4, c_su4),
                   (sl4, c_sl4), (iu4, c_iu4), (bi, c_bi)):
        kb.dma(t_[:], c_)
    kb.dma(bm[:], c_bm.rearrange("p (s t) -> p s t", s=16))
    kb.dma(idb[:], c_id, q='pool')
    kb.memset(onb[:], 1.0); kb.memset(onf[:], 1.0)
    epsc = sb("epsc", [128, 1]); kb.memset(epsc[:], 1e-6)
    sq = [sb("sq%d" % i, [128, 512], BF16) for i in range(2)]
    rstd = sb("rstd", [128, 512])
    NWT = 6
    wt = [sb("wt%d" % i, [128, 8, 128], BF16) for i in range(NWT)]
    wrr = [0]
    pa = [ps("pa%d" % i, [128, 512]) for i in range(2)]
    parr = [0]
    pn = ps("pn", [128, 512]); pS = ps("pS", [128, 512]); pC = ps("pC", [128, 512]); pD = ps("pD", [128, 512]); pN = ps("pN", [128, 512])
    pT = ps("pT", [128, 1024], BF16)
    lnc = sb("lnc", [128, 8]); ln2c = sb("ln2c", [128, 8]); lnfc = sb("lnfc", [128, 8])
    tokm = sb("tokm", [128, 512]); junk = sb("junk", [128, 128]); st1 = sb("st1", [128, 8])
    sinkc = sb("sinkc", [128, 8]); cw = sb("cw", [128, 60, 4])
    dtb_t = sb("dtb_t", [128, 4]); negA = sb("negA", [128, 4]); nw_t = sb("nw_t", [128, 128])
    wab = sb("wab", [128, 8, 8], BF16); wz = sb("wz", [128, 8, 512], BF16)
    Sst = [sb("Sst%d" % i, [128, 128]) for i in range(4)]
    xh = [sb("xh%d" % i, [128, 8, HALO]) for i in range(2)]
    xm2 = [sb("xm2_%d" % i, [128, 8, 2]) for i in range(2)]

    def wload(W2, k0, nk, c0, ncol, dup=False):
        t = wt[wrr[0] % NWT]; wrr[0] += 1
        if not dup:
            src = W2[k0 * 128:(k0 + nk) * 128, c0:c0 + ncol].rearrange("(k p) n -> p k n", p=128)
            kb.dma(t[:, 0:nk, 0:ncol], src, q='pool')
            return t[:, 0:nk, 0:ncol]
        t2 = wt[wrr[0] % NWT]; wrr[0] += 1
        t3 = wt[wrr[0] % NWT]; wrr[0] += 1
        src = W2[k0 * 128:(k0 + nk) * 128, c0:c0 + 128].rearrange("(k p) n -> p k n", p=128)
        kb.dma(t[:, 0:nk, 0:128], src, q='pool')
        for hh, td in enumerate((t2, t3)):
            kb.cp(td[:, 0:nk, 0:64], t[:, 0:nk, hh * 64:(hh + 1) * 64])
            if KVAR == 1:
                kb.cp(td[:, 0:nk, 64:128], t[:, 0:nk, hh * 64:(hh + 1) * 64])
            else:
                kb.act(td[:, 0:nk, 64:128], t[:, 0:nk, hh * 64:(hh + 1) * 64], AF.Copy)
        return t2[:, 0:nk, 0:128], t3[:, 0:nk, 0:128]

    def tiles(c0, c1, step=512):
        out = []
        c = c0
        while c < c1:
            out.append((c, min(step, c1 - c))); c += step
        return out

    def proj(W2, nk, c0, ncol, rhs_fn, cols, evac, w=None):
        if w is None:
            w = wload(W2, 0, nk, c0, ncol)
        M = w.shape[2]
        for (t0, n) in cols:
            p = pa[parr[0] % 2]; parr[0] += 1
            for k in range(nk):
                kb.mm(p[0:M, 0:n], w[:, k, :], rhs_fn(k, t0, n), start=(k == 0), stop=(k == nk - 1))
            evac(p[0:M, 0:n], t0, n)

    def rmsnorm(src_fn, wcol, ncols, dst_fn):
        for (t0, n) in tiles(0, ncols):
            for c in range(8):
                s = sq[c % 2]
                kb.act(s[:, 0:n], src_fn(c, t0, n), AF.Square)
                kb.mm(pn[:, 0:n], onb[:], s[:, 0:n], start=(c == 0), stop=(c == 7))
            kb.act(rstd[:, 0:n], pn[:, 0:n], AF.Sqrt, bias=epsc[:, 0:1], scale=1.0 / D)
            kb.rcp(rstd[:, 0:n], rstd[:, 0:n])
            for c in range(8):
                kb.stt(dst_fn(c, t0, n), src_fn(c, t0, n), wcol[:, c:c + 1], rstd[:, 0:n], ALU.mult, ALU.mult)

    def loadT(dst, src2d, R):
        kb.dma(tokm[0:R, 0:128], src2d)
        kb.tr(pC[:, 0:R], tokm[0:R, 0:128], idf[0:R, 0:R])
        kb.cp(dst, pC[:, 0:R])

    def conv_fm(dst, src, wc, K, n):
        kb.ts(dst, src[:, 0:n], wc[:, 0:1], None, op0=ALU.mult)
        for j in range(1, K):
            kb.stt(dst, src[:, j:j + n], wc[:, j:j + 1], dst, ALU.mult, ALU.add)

    def conv_smp(dst3, src3, wc, K):
        kb.ts(dst3, src3[:, :, 0:4], wc[:, 0:1], None, op0=ALU.mult)
        for j in range(1, K):
            kb.stt(dst3, src3[:, :, j:j + 4], wc[:, j:j + 1], dst3, ALU.mult, ALU.add)

    def v3(ap2, k):
        return ap2.rearrange("p (s j) -> p s j", j=k)

    def load_state_T(st_l, K1, feat0, dst3):
        nr = NSQ * K1
        kb.dma(tokm[0:nr, 0:128], st_l[:, feat0:feat0 + 128])
        kb.tr(pC[:, 0:nr], tokm[0:nr, 0:128], idf[0:nr, 0:nr])
        kb.cp(dst3, v3(pC[:, 0:nr], K1))

    def store_state_T(outp_l, outs_l, feat0, K1, prow, smp3):
        kb.tr(pC[0:K1, 0:128], prow, idf[:])
        kb.cp(tokm[0:K1, 0:128], pC[0:K1, 0:128])
        kb.dma(outp_l[:, feat0:feat0 + 128], tokm[0:K1, 0:128])
        nr = NSQ * K1
        kb.cp(v3(junk[:, 0:nr], K1), smp3)
        kb.tr(pD[0:nr, 0:128], junk[:, 0:nr], idf[:])
        kb.cp(tokm[0:nr, 128:256], pD[0:nr, 0:128])
        kb.dma(outs_l.rearrange("s j f -> (s j) f")[:, feat0:feat0 + 128], tokm[0:nr, 128:256])

    nt_all = SEQ // 128
    with kb.phase():
        tin = [sb("tin%d" % i, [128, D]) for i in range(2)]
        tout = [sb("tout%d" % i, [128, 8, 128]) for i in range(2)]
        for t in range(nt_all + 1):
            ti = tin[t % 2]; to = tout[t % 2]
            rows = 128 if t < nt_all else NS
            src = xp[t * 128:(t + 1) * 128, :] if t < nt_all else xsm
            kb.dma(ti[0:rows, :], src)
            for c in range(8):
                pp = pC if c % 2 == 0 else pD
                kb.tr(pp[:, 0:rows], ti[0:rows, c * 128:(c + 1) * 128], idf[0:rows, 0:rows])
                if c % 2 == 0:
                    kb.cp(to[:, c, 0:rows], pp[:, 0:rows])
                else:
                    kb.act(to[:, c, 0:rows], pp[:, 0:rows], AF.Copy)
            kb.dma(xbuf[0][:, :, t * 128:t * 128 + rows], to[:, :, 0:rows])
    loadT(lnfc[:], lnf, 8)
    if STOP <= 0:
        kb.finish(); return nc, kb

    for l in range(L):
        xin = xbuf[l % 2]; xout = xbuf[(l + 1) % 2]
        W = w_in[l]
        loadT(lnc[:], ln1[l], 8); loadT(ln2c[:], ln2[l], 8)
        kb.dma(sinkc[:], sinks[l:l + 1, :].to_broadcast([128, 8]))
        for c in range(4):
            loadT(cw[:, c, 0:3], scw[l][:, c * 128:(c + 1) * 128], 3)
        for c in range(12):
            loadT(cw[:, 4 + c, 0:4], dncw[l][:, c * 128:(c + 1) * 128], 4)
        for c in range(44):
            loadT(cw[:, 16 + c, 0:3], fcw[l][:, c * 128:(c + 1) * 128], 3)
        kb.dma(dtb_t[:], dtb[l:l + 1, :].to_broadcast([128, 4]))
        kb.dma(negA[:], alog[l:l + 1, :].to_broadcast([128, 4]))
        kb.act(negA[:], negA[:], AF.Exp)
        kb.ts(negA[:], negA[:], -1.0, None, op0=ALU.mult)
        kb.dma(nw_t[:], dnw[l:l + 1, :].to_broadcast([128, 128]))
        kb.dma(wab[:], W[:, A0:A0 + 8].rearrange("(k p) n -> p k n", p=128), q='pool')
        kb.dma(wz[:], W[:, Z0:Z0 + 512].rearrange("(k p) n -> p k n", p=128), q='pool')
        for hd in range(4):
            kb.memset(Sst[hd][:], 0.0)

        for b in range(NBLK):
            last = (b == NBLK - 1)
            ns = NS if last else 0
            xc = NPB + ns; hc = HALO + xc
            col0 = b * NPB
            ntp = NPB // 128
            with kb.phase():
                xb_ = sb("xblk", [128, 8, XC]); h = sb("h", [128, 8, HC], BF16)
                kb.dma(xb_[:, :, 0:NPB], xin[:, :, col0:col0 + NPB])
                if last:
                    kb.dma(xb_[:, :, NPB:NPB + NS], xin[:, :, SEQ:SEQ + NS])
                xhc = xh[b % 2]; xhn = xh[(b + 1) % 2]
                if b == 0:
                    kb.memset(xhc[:], 0.0)
                kb.cp(xhn[:], xb_[:, :, NPB - HALO:NPB])
                rmsnorm(lambda c, t0, n: xhc[:, c, t0:t0 + n], lnc, HALO, lambda c, t0, n: h[:, c, t0:t0 + n])
                rmsnorm(lambda c, t0, n: xb_[:, c, t0:t0 + n], lnc, xc, lambda c, t0, n: h[:, c, HALO + t0:HALO + t0 + n])
                if STOP <= 1:
                    kb.finish(); return nc, kb
                hr = lambda k, t0, n: h[:, k, t0:t0 + n]
                own_h = tiles(HALO, hc)
                own = tiles(0, xc)

                with kb.phase():
                    brT = sb("brT", [128, 4, XC], BF16); merged = sb("merged", [128, 8, XC], BF16)
                    gate = sb("gate", [128, XC], BF16); gtmp = sb("gtmp", [128, 512])

                    def merge(br, Wb):
                        for oc_ in range(8):
                            proj(W, 8, G0 + br * 1024 + oc_ * 128, 128, hr, own_h,
                                 lambda p, t0, n: kb.act(gate[:, t0 - HALO:t0 - HALO + n], p, AF.Sigmoid))

                            def ev(p, t0, n, oc_=oc_):
                                if br == 0:
                                    kb.tt(merged[:, oc_, t0:t0 + n], p, gate[:, t0:t0 + n], ALU.mult)
                                else:
                                    kb.tt(gtmp[:, 0:n], p, gate[:, t0:t0 + n], ALU.mult)
                                    kb.tt(merged[:, oc_, t0:t0 + n], merged[:, oc_, t0:t0 + n], gtmp[:, 0:n], ALU.add)
                            proj(Wb, 4, oc_ * 128, 128, lambda k, t0, n: brT[:, k, t0:t0 + n], own, ev)

                    with kb.phase():
                        qT = sb("qT", [128, 4, XC], BF16); kT2 = [sb("kT2_%d" % i, [128, HC], BF16) for i in range(2)]
                        NTL = HC // 128 + 1
                        v2 = sb("v2", [128, NTL, 2, 128], BF16)
                        sc_s = sb("sc_s", [128, 256]); sc_e = sb("sc_e", [128, 256]); sc_p = sb("sc_p", [128, 256], BF16)
                        sc_p2 = sb("sc_p2", [4, 64], BF16); pTs = sb("pTs", [128, 2, 128], BF16)
                        sc_sB = sb("sc_sB", [128, 256]); sc_eB = sb("sc_eB", [128, 256]); sc_pB = sb("sc_pB", [128, 256], BF16)
                        pTsB = sb("pTsB", [128, 2, 128], BF16); st1B = sb("st1B", [128, 8])
                        kcd = sb("kcd", [128, 2, 128], BF16); vcd = sb("vcd", [128, 2, 128], BF16); kcT = sb("kcT", [128, 2, 128], BF16)
                        kvo = sb("kvo", [128, 2, 128]); kvf = sb("kvf", [128, 2, 192])
                        for c in range(4):
                            proj(W, 8, Q0 + c * 128, 128, hr, own_h,
                                 lambda p, t0, n, c=c: kb.act(qT[:, c, t0 - HALO:t0 - HALO + n], p, AF.Copy, scale=0.125))
                        if STOP <= 1.05:
                            kb.finish(); return nc, kb
                        R0 = HALO + NPB - 128
                        wk2 = wload(W, 0, 8, K0, 128, dup=True) if KVAR != 4 else (wload(W, 0, 8, K0, 128), wload(W, 0, 8, K0, 128))
                        for kvh in range(2):
                            def ev_k(p, t0, n, kvh=kvh):
                                if KVAR == 5:
                                    kb.act(kT2[kvh][:, t0:t0 + n], p, AF.Copy)
                                elif KVAR == 6:
                                    kb.ts(kT2[kvh][:, t0:t0 + n], p, 1.0, None, op0=ALU.mult)
                                else:
                                    kb.cp(kT2[kvh][:, t0:t0 + n], p)
                                if last and KVAR != 2:
                                    lo = max(t0, R0); hi = t0 + n
                                    if hi > lo:
                                        kb.act(kvf[:, kvh, lo - R0:hi - R0], p[:, lo - t0:hi - t0], AF.Copy)
                            proj(W, 8, K0, 128, hr, (own_h if KVAR == 3 else tiles(0, hc)), ev_k, w=wk2[kvh])
                        if STOP <= 1.1:
                            kb.finish(); return nc, kb
                        ntile = (hc + 127) // 128
                        wv2 = wload(W, 0, 8, V0, 128, dup=True)
                        for kvh in range(2):
                            w = wv2[kvh]
                            for t in range(ntile):
                                n = min(128, hc - t * 128)
                                p = pa[parr[0] % 2]; parr[0] += 1
                                for k in range(8):
                                    kb.mm(p[0:n, 0:128], h[:, k, t * 128:t * 128 + n], w[:, k, :], start=(k == 0), stop=(k == 7))
                                kb.cp(v2[0:n, t, kvh, :], p[0:n, 0:128])
                                if last and t >= ntile - 2 and STOP > 1.25:
                                    kb.act(kvo[0:n, kvh, :], p[0:n, 0:128], AF.Copy)
                                    if t == ntile - 2:
                                        kb.dma(o_vp[l][:, kvh * 64:(kvh + 1) * 64], kvo[0:128, kvh, 0:64])
                                    else:
                                        kb.dma(o_vs[l].rearrange("s (a i) f -> s a i f", i=4)[:, 31, :, kvh * 64:(kvh + 1) * 64],
                                               kvo[0:NS, kvh, 0:64])
                        if STOP <= 1.3:
                            kb.finish(); return nc, kb
                        if last:
                            for kvh in range(2):
                                kb.tr(pC[:, 0:128], kvf[:, kvh, 0:128], idf[:])
                                kb.cp(tokm[:, 0:64], pC[:, 0:64])
                                kb.dma(o_kp[l][:, kvh * 64:(kvh + 1) * 64], tokm[:, 0:64])
                                kb.tr(pD[0:NS, 0:128], kvf[:, kvh, 128:192], idf[:])
                                kb.cp(tokm[0:NS, 64:128], pD[0:NS, 0:64])
                                kb.dma(o_ks[l].rearrange("s (a i) f -> s a i f", i=4)[:, 31, :, kvh * 64:(kvh + 1) * 64],
                                       tokm[0:NS, 64:128])
                            kb.dma(o_ks[l][:, 0:124, :], ck[l][:, 4:128, :])
                            kb.dma(o_vs[l][:, 0:124, :], cv[l][:, 4:128, :])

                        if STOP <= 1.5:
                            kb.finish(); return nc, kb
                        def softmax(nq, NK, mask_ap, sink_ap, pS=pS, sc_s=sc_s, sc_e=sc_e, st1=st1):
                            kb.tt(sc_s[0:nq, 0:NK], pS[0:nq, 0:NK], mask_ap, ALU.add)
                            kb.rmax(st1[0:nq, 0:1], sc_s[0:nq, 0:NK])
                            kb.ts(st1[0:nq, 1:2], st1[0:nq, 0:1], sink_ap, -1.0, op0=ALU.max, op1=ALU.mult)
                            kb.act(sc_e[0:nq, 0:NK], sc_s[0:nq, 0:NK], AF.Exp, bias=st1[0:nq, 1:2], scale=1.0)
                            kb.rsum(st1[0:nq, 2:3], sc_e[0:nq, 0:NK])
                            kb.act(st1[0:nq, 3:4], sink_ap, AF.Exp, bias=st1[0:nq, 1:2], scale=1.0)
                            kb.tt(st1[0:nq, 4:5], st1[0:nq, 2:3], st1[0:nq, 3:4], ALU.add)
                            kb.rcp(st1[0:nq, 5:6], st1[0:nq, 4:5])

                        for tq in range(ntp):
                            for hq in range(8):
                                c = hq // 2; pb = (hq % 2) * 64; kvh = hq // 4
                                par = hq % 2
                                pS_, pC_ = (pS, pC) if par == 0 else (pN, pD)
                                scs_, sce_, scp_, pTs_, st_ = (sc_s, sc_e, sc_p, pTs, st1) if par == 0 else (sc_sB, sc_eB, sc_pB, pTsB, st1B)
                                kb.mm(pS_[:, 0:256], qT[pb:pb + 64, c, tq * 128:(tq + 1) * 128],
                                      kT2[kvh][pb:pb + 64, tq * 128:tq * 128 + 256])
                                m = mp0 if (b == 0 and tq == 0) else mp
                                softmax(128, 256, m[:, :], sinkc[:, hq:hq + 1], pS=pS_, sc_s=scs_, sc_e=sce_, st1=st_)
                                kb.ts(scp_[:, 0:256], sce_[:, 0:256], st_[:, 5:6], None, op0=ALU.mult)
                                for i in range(2):
                                    kb.tr(pT[:, par * 256 + i * 128:par * 256 + (i + 1) * 128], scp_[:, i * 128:(i + 1) * 128], idb[:])
                                    kb.cp(pTs_[:, i, :], pT[:, par * 256 + i * 128:par * 256 + (i + 1) * 128])
                                for i in range(2):
                                    kb.mm(pC_[:, 0:128], v2[:, tq + i, kvh, :], pTs_[:, i, :], start=(i == 0), stop=(i == 1))
                                kb.act(brT[pb:pb + 64, c, tq * 128:(tq + 1) * 128], pC_[pb:pb + 64, 0:128], AF.Copy)
                        if STOP <= 1.7:
                            kb.finish(); return nc, kb
                        if last:
                            ts_ = ntile - 1
                            for s in range(NSQ):
                                ksrc = ck[l, s].rearrange("w (h d) -> w h d", h=2)
                                vsrc = cv[l, s].rearrange("w (h d) -> w h d", h=2)
                                kb.dma(kcd[:, :, 0:64], ksrc, q='pool'); kb.dma(kcd[:, :, 64:128], ksrc, q='pool')
                                kb.dma(vcd[:, :, 0:64], vsrc, q='pool'); kb.dma(vcd[:, :, 64:128], vsrc, q='pool')
                                for kvh in range(2):
                                    kb.tr(pT[:, 512 + kvh * 128:512 + (kvh + 1) * 128], kcd[:, kvh, :], idb[:])
                                    kb.cp(kcT[:, kvh, :], pT[:, 512 + kvh * 128:512 + (kvh + 1) * 128])
                                kb.memset(sc_p2[:], 0.0)
                                sc0 = NPB + s * 4
                                for hq in range(8):
                                    c = hq // 2; pb = (hq % 2) * 64; kvh = hq // 4
                                    kb.mm(pS[0:4, 0:128], qT[pb:pb + 64, c, sc0:sc0 + 4], kcT[pb:pb + 64, kvh, :])
                                    kb.mm(pS[0:4, 128:132], qT[pb:pb + 64, c, sc0:sc0 + 4], kT2[kvh][pb:pb + 64, HALO + sc0:HALO + sc0 + 4])
                                    softmax(4, 132, msm[:, :], sinkc[0:4, hq:hq + 1])
                                    kb.ts(sc_p[0:4, 0:128], sc_e[0:4, 0:128], st1[0:4, 5:6], None, op0=ALU.mult)
                                    kb.ts(sc_p2[0:4, s * 4:s * 4 + 4], sc_e[0:4, 128:132], st1[0:4, 5:6], None, op0=ALU.mult)
                                    kb.tr(pT[:, 0:4], sc_p[0:4, 0:128], idb[0:4, 0:4])
                                    kb.cp(pTs[:, 0, 0:4], pT[:, 0:4])
                                    kb.tr(pT[0:64, 128:132], sc_p2[0:4, 0:64], idb[0:4, 0:4])
                                    kb.cp(pTs[0:64, 1, 0:4], pT[0:64, 128:132])
                                    kb.mm(pC[:, 0:4], vcd[:, kvh, :], pTs[:, 0, 0:4], start=True, stop=False)
                                    kb.mm(pC[:, 0:4], v2[0:64, ts_, kvh, :], pTs[0:64, 1, 0:4], start=False, stop=True)
                                    kb.act(brT[pb:pb + 64, c, sc0:sc0 + 4], pC[pb:pb + 64, 0:4], AF.Copy)
                    if STOP <= 2:
                        kb.finish(); return nc, kb
                    merge(0, wba[l])
                    if STOP <= 3:
                        kb.finish(); return nc, kb

                    with kb.phase():
                        t_sh = sb("t_sh", [128, 2 + XC]); t_pr = sb("t_pr", [128, 2 + XC]); t_cu = sb("t_cu", [128, XC])
                        smA = sb("smA", [128, NSQ, 6])
                        for c in range(4):
                            cols2 = tiles(HALO - 2, hc)
                            proj(W, 8, SH0 + c * 128, 128, hr, cols2,
                                 lambda p, t0, n: kb.act(t_sh[:, t0 - (HALO - 2):t0 - (HALO - 2) + n], p, AF.Copy))
                            proj(W, 8, SC0 + c * 128, 128, hr, cols2,
                                 lambda p, t0, n: kb.tt(t_pr[:, t0 - (HALO - 2):t0 - (HALO - 2) + n], p,
                                                        t_sh[:, t0 - (HALO - 2):t0 - (HALO - 2) + n], ALU.mult))
                            conv_fm(t_cu[:, 0:NPB], t_pr, cw[:, c, :], 3, NPB)
                            if last:
                                load_state_T(st_sc[l], 2, c * 128, smA[:, :, 0:2])
                                kb.cp(smA[:, :, 2:6], v3(t_pr[:, 2 + NPB:2 + NPB + NS], 4))
                                conv_smp(v3(t_cu[:, NPB:NPB + NS], 4), smA, cw[:, c, :], 3)
                                store_state_T(o_scp[l], o_scs[l], c * 128, 2, t_pr[:, NPB:NPB + 2], smA[:, :, 4:6])
                            proj(W, 8, SB0 + c * 128, 128, hr, own_h,
                                 lambda p, t0, n, c=c: kb.tt(brT[:, c, t0 - HALO:t0 - HALO + n], p, t_cu[:, t0 - HALO:t0 - HALO + n], ALU.mult))
                    if STOP <= 4:
                        kb.finish(); return nc, kb
                    merge(1, wbb[l])

                    with kb.phase():
                        rw = sb("rw", [128, 3 + XC]); t1 = sb("t1", [128, 512]); smA = sb("smA", [128, NSQ, 7])
                        qn = sb("qn", [128, XC]); kn = sb("kn", [128, XC]); vv = sb("vv", [128, XC])
                        ntl = ntp + (1 if last else 0)
                        dsc = sb("dsc", [128, ntl, 28]); egl = sb("egl", [128, ntp, 2, 4]); egls = sb("egls", [128, 4, 16])
                        Gs = sb("Gs", [64, 4, 16])
                        kbT = sb("kbT", [128, 128]); Xs = sb("Xs", [128, 128]); Ys = sb("Ys", [128, 128])
                        Bs = [sb("Bs%d" % i, [128, 128]) for i in range(2)]; Cs = [sb("Cs%d" % i, [128, 128]) for i in range(2)]
                        Rs = [sb("Rs%d" % i, [128, 128]) for i in range(2)]; Rts = [sb("Rt%d" % i, [128, 128]) for i in range(2)]
                        Kbk = sb("Kbk", [128, 128]); Kdk = sb("Kdk", [128, 128]); Xv = sb("Xv", [128, 128])
                        WT = sb("WT", [128, 128]); Us = sb("Us", [128, 128]); QKT = sb("QKT", [128, 128]); vp = sb("vp", [128, 128])
                        osb = sb("osb", [128, 128]); zs = sb("zs", [128, 128]); ocb = sb("ocb", [128, 128], BF16)
                        jk2 = sb("jk2", [128, 128]); tA = sb("tA", [128, 128]); tB = sb("tB", [128, 128])
                        DTi = sb("DTi", [128, 128]); Di = sb("Di", [128, 128]); egrow = sb("egrow", [128, 128]); qt = sb("qt", [128, 128])
                        if last:
                            S_all = sb("S_all", [128, NSQ, 128]); WTm = sb("WTm", [128, NSQ, 64]); QTm = sb("QTm", [128, NSQ, 64])
                            Ktm = sb("Ktm", [64, NSQ, 128])
                        for t in range(ntl):
                            smp_t = (t == ntp)
                            NT = 64 if smp_t else 128
                            c0 = t * 128
                            ds = dsc[0:NT, t, :]
                            ium = iu4 if smp_t else iu
                            bom = bo4 if smp_t else bo
                            for k in range(8):
                                kb.mm(pN[0:NT, 0:8], h[:, k, HALO + c0:HALO + c0 + NT], wab[:, k, :], start=(k == 0), stop=(k == 7))
                            kb.tt(ds[:, 0:4], pN[0:NT, 0:4], dtb_t[0:NT, :], ALU.add)
                            kb.act(ds[:, 0:4], ds[:, 0:4], AF.Exp)
                            kb.act(ds[:, 0:4], ds[:, 0:4], AF.Ln, bias=1.0, scale=1.0)
                            kb.tt(ds[:, 4:8], ds[:, 0:4], negA[0:NT, :], ALU.mult)
                            kb.act(ds[:, 8:12], pN[0:NT, 4:8], AF.Sigmoid)
                            kb.mm(pD[0:NT, 0:4], ium[0:NT, 0:NT], ds[:, 4:8])
                            kb.mm(pD[0:NT, 4:8], bom[0:NT, 0:NT], ds[:, 4:8])
                            kb.cp(ds[:, 12:16], pD[0:NT, 0:4])
                            kb.ts(ds[:, 16:20], pD[0:NT, 0:4], -1.0, None, op0=ALU.mult)
                            kb.act(ds[:, 20:24], ds[:, 12:16], AF.Exp)
                            kb.tt(ds[:, 20:24], ds[:, 20:24], ds[:, 8:12], ALU.mult)
                            kb.tt(ds[:, 24:28], pD[0:NT, 4:8], ds[:, 12:16], ALU.subtract)
                            kb.act(ds[:, 24:28], ds[:, 24:28], AF.Exp)
                            if not smp_t:
                                for x in range(2):
                                    kb.mm(pD[:, 8 + 4 * x:12 + 4 * x], onf[64 * x:64 * x + 64, :], dsc[64 * x:64 * x + 64, t, 4:8])
                                    kb.act(egl[:, t, x, :], pD[:, 8 + 4 * x:12 + 4 * x], AF.Exp)
                            else:
                                kb.tt(Gs[:], bi[:, :].unsqueeze(1).to_broadcast([64, 4, 16]),
                                      dsc[0:64, t, 4:8].unsqueeze(2).to_broadcast([64, 4, 16]), ALU.mult)
                                kb.mm(pD[:, 16:80], onf[0:64, :], Gs[:].rearrange("p h s -> p (h s)"))
                                kb.act(egls[:].rearrange("p h s -> p (h s)"), pD[:, 16:80], AF.Exp)

                        for hd in range(4):
                            for part, dst in enumerate((qn, kn, vv)):
                                fo = part * 512 + hd * 128
                                ci = 4 + part * 4 + hd
                                proj(W, 8, DN0 + fo, 128, hr, tiles(HALO - 3, hc),
                                     lambda p, t0, n: kb.act(rw[:, t0 - (HALO - 3):t0 - (HALO - 3) + n], p, AF.Copy))
                                conv_fm(dst[:, 0:NPB], rw, cw[:, ci, :], 4, NPB)
                                if last:
                                    load_state_T(st_dnc[l], 3, fo, smA[:, :, 0:3])
                                    kb.cp(smA[:, :, 3:7], v3(rw[:, 3 + NPB:3 + NPB + NS], 4))
                                    conv_smp(v3(dst[:, NPB:NPB + NS], 4), smA, cw[:, ci, :], 4)
                                    store_state_T(o_dncp[l], o_dncs[l], fo, 3, rw[:, NPB:NPB + 3], smA[:, :, 4:7])
                                kb.act(dst[:, 0:xc], dst[:, 0:xc], AF.Silu)
                                if part < 2:
                                    for (t0, n) in tiles(0, xc):
                                        kb.act(t1[:, 0:n], dst[:, t0:t0 + n], AF.Square)
                                        kb.mm(pn[:, 0:n], onf[:], t1[:, 0:n])
                                        kb.act(rstd[:, 0:n], pn[:, 0:n], AF.Sqrt, bias=epsc[:, 0:1], scale=1.0)
                                        kb.rcp(rstd[:, 0:n], rstd[:, 0:n])
                                        if part == 0:
                                            kb.stt(dst[:, t0:t0 + n], dst[:, t0:t0 + n], 128.0 ** -0.5, rstd[:, 0:n], ALU.mult, ALU.mult)
                                        else:
                                            kb.tt(dst[:, t0:t0 + n], dst[:, t0:t0 + n], rstd[:, 0:n], ALU.mult)
                            for t in range(ntl):
                                smp_t = (t == ntp)
                                NT = 64 if smp_t else 128
                                c0 = t * 128
                                cs = slice(c0, c0 + NT)
                                sum_, slm, mniu_, mnil_ = (su4, sl4, mniu4, mnil4) if smp_t else (su, sl, mniu, mnil)
                                levels = 1 if smp_t else 5
                                rr = slice(0, NT)
                                col = lambda j: dsc[0:NT, t, j + hd:j + hd + 1]
                                kb.mm(pN[:, 0:NT], col(8).to_broadcast([NT, 128]), idf[0:NT, 0:NT])
                                kb.tt(kbT[:, 0:NT], kn[:, cs], pN[:, 0:NT], ALU.mult)
                                kb.mm(pN[:, 128:128 + NT], col(12).to_broadcast([NT, 128]), idf[0:NT, 0:NT])
                                kb.tt(tA[rr, 0:NT], pN[0:NT, 128:128 + NT], mniu_[0:NT, 0:NT], ALU.add)
                                kb.act(DTi[rr, 0:NT], tA[rr, 0:NT], AF.Exp, bias=col(16), scale=1.0)
                                kb.stt(tB[rr, 0:NT], pN[0:NT, 128:128 + NT], -1.0, mnil_[0:NT, 0:NT], ALU.mult, ALU.add)
                                kb.act(Di[rr, 0:NT], tB[rr, 0:NT], AF.Exp, bias=col(12), scale=1.0)
                                kb.act(egrow[:, 0:NT], pN[:, 128:128 + NT], AF.Exp)
                                kb.tt(qt[:, 0:NT], qn[:, cs], egrow[:, 0:NT], ALU.mult)
                                kb.mm(pS[0:NT, 0:NT], kn[:, cs], kbT[:, 0:NT])
                                kb.tt(Xs[rr, 0:NT], pS[0:NT, 0:NT], sum_[0:NT, 0:NT], ALU.mult)
                                kb.tt(Xs[rr, 0:NT], Xs[rr, 0:NT], DTi[rr, 0:NT], ALU.mult)
                                kb.mm(pS[0:NT, 128:128 + NT], kbT[:, 0:NT], kn[:, cs])
                                kb.tt(Ys[rr, 0:NT], pS[0:NT, 128:128 + NT], slm[0:NT, 0:NT], ALU.mult)
                                kb.tt(Ys[rr, 0:NT], Ys[rr, 0:NT], Di[rr, 0:NT], ALU.mult)
                                kb.tt(Rs[0][rr, 0:NT], idf[0:NT, 0:NT], Xs[rr, 0:NT], ALU.subtract)
                                kb.tt(Rts[0][rr, 0:NT], idf[0:NT, 0:NT], Ys[rr, 0:NT], ALU.subtract)
                                Bc, Cc, r = Xs, Ys, 0
                                for nlv in range(levels):
                                    lastlv = (nlv == levels - 1)
                                    Bn = Bs[nlv % 2]; Cn = Cs[nlv % 2]
                                    kb.mm(pN[0:NT, 0:NT], Cc[rr, 0:NT], Bc[rr, 0:NT])
                                    kb.cp(Bn[rr, 0:NT], pN[0:NT, 0:NT])
                                    if not lastlv:
                                        kb.mm(pN[0:NT, 128:128 + NT], Bc[rr, 0:NT], Cc[rr, 0:NT])
                                        kb.cp(Cn[rr, 0:NT], pN[0:NT, 128:128 + NT])
                                    kb.mm(pS[0:NT, 0:NT], Rts[r][rr, 0:NT], Bn[rr, 0:NT])
                                    kb.tt(Rs[1 - r][rr, 0:NT], Rs[r][rr, 0:NT], pS[0:NT, 0:NT], ALU.add)
                                    if not lastlv:
                                        kb.mm(pS[0:NT, 128:128 + NT], Bn[rr, 0:NT], Rts[r][rr, 0:NT])
                                        kb.tt(Rts[1 - r][rr, 0:NT], Rts[r][rr, 0:NT], pS[0:NT, 128:128 + NT], ALU.add)
                                    Bc, Cc, r = Bn, Cn, 1 - r
                                TT = Rs[r]
                                kb.tr(pC[0:NT, 0:128], kn[:, cs], idf[:])
                                kb.ts(Kbk[rr, :], pC[0:NT, 0:128], col(20), None, op0=ALU.mult)
                                kb.ts(Kdk[rr, :], pC[0:NT, 0:128], col(24), None, op0=ALU.mult)
                                kb.tr(pD[0:NT, 0:128], vv[:, cs], idf[:])
                                kb.ts(Xv[rr, :], pD[0:NT, 0:128], col(8), None, op0=ALU.mult)
                                kb.mm(pC[:, 128:128 + NT], Kbk[rr, :], TT[rr, 0:NT])
                                kb.cp(WT[:, 0:NT], pC[:, 128:128 + NT])
                                kb.mm(pD[0:NT, 128:256], TT[rr, 0:NT], Xv[rr, :])
                                kb.cp(Us[rr, :], pD[0:NT, 128:256])
                                kb.mm(pS[0:NT, 256:256 + NT], kn[:, cs], qn[:, cs])
                                kb.tt(QKT[rr, 0:NT], pS[0:NT, 256:256 + NT], DTi[rr, 0:NT], ALU.mult)
                                pz = pa[parr[0] % 2]; parr[0] += 1
                                for k in range(8):
                                    kb.mm(pz[0:NT, 0:128], h[:, k, HALO + c0:HALO + c0 + NT], wz[:, k, hd * 128:(hd + 1) * 128],
                                          start=(k == 0), stop=(k == 7))
                                kb.act(zs[rr, :], pz[0:NT, 0:128], AF.Silu)
                                if not smp_t:
                                    S = Sst[hd]
                                    for x in range(2):
                                        r_ = slice(64 * x, 64 * x + 64)
                                        kb.mm(pC[:, 0:128], WT[:, 0:128], S[:])
                                        kb.tt(vp[r_, :], Us[r_, :], pC[r_, 0:128], ALU.subtract)
                                        kb.mm(pD[:, 0:128], qt[:, 0:128], S[:], start=True, stop=False)
                                        kb.mm(pD[:, 0:128], QKT[r_, 0:128], vp[r_, :], start=False, stop=True)
                                        kb.act(osb[r_, :], pD[r_, 0:128], AF.Copy)
                                        kb.mm(pC[:, 256:384], Kdk[r_, :], vp[r_, :])
                                        kb.stt(S[:], S[:], egl[:, t, x, hd:hd + 1], pC[:, 256:384], ALU.mult, ALU.add)
                                else:
                                    kb.dma(S_all[:], st_dn[l, :, hd].rearrange("s d e -> d s e"))
                                    kb.tt(WTm[:], WT[:, 0:64].unsqueeze(1).to_broadcast([128, NSQ, 64]), bm[:], ALU.mult)
                                    kb.tt(QTm[:], qt[:, 0:64].unsqueeze(1).to_broadcast([128, NSQ, 64]), bm[:], ALU.mult)
                                    kb.tt(Ktm[:], Kdk[0:64, :].unsqueeze(1).to_broadcast([64, NSQ, 128]),
                                          bi[:, :].unsqueeze(2).to_broadcast([64, NSQ, 128]), ALU.mult)
                                    for s_ in range(NSQ):
                                        kb.mm(pC[0:64, 0:128], WTm[:, s_, :], S_all[:, s_, :], start=(s_ == 0), stop=(s_ == NSQ - 1))
                                    kb.tt(vp[0:64, :], Us[0:64, :], pC[0:64, 0:128], ALU.subtract)
                                    for s_ in range(NSQ):
                                        kb.mm(pD[0:64, 0:128], QTm[:, s_, :], S_all[:, s_, :], start=(s_ == 0), stop=False)
                                    kb.mm(pD[0:64, 0:128], QKT[0:64, 0:64], vp[0:64, :], start=False, stop=True)
                                    kb.act(osb[0:64, :], pD[0:64, 0:128], AF.Copy)
                                    for s_ in range(NSQ):
                                        kb.mm(pC[:, 256:384], Ktm[:, s_, :], vp[0:64, :])
                                        kb.stt(S_all[:, s_, :], S_all[:, s_, :], egls[:, hd, s_:s_ + 1], pC[:, 256:384], ALU.mult, ALU.add)
                                    kb.dma(o_dns[l, :, hd].rearrange("s d e -> d s e"), S_all[:])
                                kb.act(jk2[rr, :], osb[rr, :], AF.Square)
                                kb.rsum(st1[rr, 0:1], jk2[rr, :])
                                kb.act(st1[rr, 1:2], st1[rr, 0:1], AF.Sqrt, bias=epsc[rr, 0:1], scale=1.0 / 128)
                                kb.rcp(st1[rr, 2:3], st1[rr, 1:2])
                                kb.stt(osb[rr, :], osb[rr, :], st1[rr, 2:3], nw_t[rr, :], ALU.mult, ALU.mult)
                                kb.tt(ocb[rr, :], osb[rr, :], zs[rr, :], ALU.mult)
                                kb.tr(pT[:, 0:NT], ocb[rr, :], idb[0:NT, 0:NT])
                                kb.cp(brT[:, hd, cs], pT[:, 0:NT])
                        if last:
                            for hd in range(4):
                                kb.dma(o_dnp[l, hd], Sst[hd][:])
                    if STOP <= 5:
                        kb.finish(); return nc, kb
                    merge(2, wbc[l])

                    for oc_ in range(8):
                        proj(wo[l], 8, oc_ * 128, 128, lambda k, t0, n: merged[:, k, t0:t0 + n], own,
                             lambda p, t0, n, oc_=oc_: kb.tt(xb_[:, oc_, t0:t0 + n], xb_[:, oc_, t0:t0 + n], p, ALU.add))

                if STOP <= 6:
                    kb.finish(); return nc, kb
                xmc = xm2[b % 2]; xmn = xm2[(b + 1) % 2]
                if b == 0:
                    kb.memset(xmc[:], 0.0)
                kb.cp(xmn[:], xb_[:, :, NPB - 2:NPB])
                rmsnorm(lambda c, t0, n: xmc[:, c, t0:t0 + n], ln2c, 2, lambda c, t0, n: h[:, c, HALO - 2 + t0:HALO - 2 + t0 + n])
                rmsnorm(lambda c, t0, n: xb_[:, c, t0:t0 + n], ln2c, xc, lambda c, t0, n: h[:, c, HALO + t0:HALO + t0 + n])
                with kb.phase():
                    act_ = sb("act", [128, 22, XC], BF16)
                    rg = sb("rg", [128, 2 + XC]); rv = sb("rv", [128, 2 + XC]); cg = sb("cg", [128, XC]); cvv = sb("cvv", [128, XC])
                    smG = sb("smG", [128, NSQ, 6]); smV = sb("smV", [128, NSQ, 6])
                    cols2 = tiles(HALO - 2, hc)
                    for c in range(22):
                        for (f0, raw, cd, smX, ci) in ((c * 128, rg, cg, smG, 16 + c), (DFF + c * 128, rv, cvv, smV, 16 + 22 + c)):
                            proj(fup[l], 8, f0, 128, hr, cols2,
                                 lambda p, t0, n, raw=raw: kb.act(raw[:, t0 - (HALO - 2):t0 - (HALO - 2) + n], p, AF.Copy))
                            conv_fm(cd[:, 0:NPB], raw, cw[:, ci, :], 3, NPB)
                            if last:
                                load_state_T(st_ff[l], 2, f0, smX[:, :, 0:2])
                                kb.cp(smX[:, :, 2:6], v3(raw[:, 2 + NPB:2 + NPB + NS], 4))
                                conv_smp(v3(cd[:, NPB:NPB + NS], 4), smX, cw[:, ci, :], 3)
                                store_state_T(o_ffp[l], o_ffs[l], f0, 2, raw[:, NPB:NPB + 2], smX[:, :, 4:6])
                        kb.act(cg[:, 0:xc], cg[:, 0:xc], AF.Silu)
                        kb.tt(act_[:, c, 0:xc], cg[:, 0:xc], cvv[:, 0:xc], ALU.mult)
                    for oc_ in range(8):
                        ws = [wload(fdn[l], k0, nk, oc_ * 128, 128) for (k0, nk) in ((0, 8), (8, 8), (16, 6))]
                        for (t0, n) in own:
                            p = pa[parr[0] % 2]; parr[0] += 1
                            for k in range(22):
                                kb.mm(p[:, 0:n], ws[k // 8][:, k % 8, :], act_[:, k, t0:t0 + n], start=(k == 0), stop=(k == 21))
                            kb.tt(xb_[:, oc_, t0:t0 + n], xb_[:, oc_, t0:t0 + n], p[:, 0:n], ALU.add)
                kb.dma(xout[:, :, col0:col0 + NPB], xb_[:, :, 0:NPB])
                if last:
                    kb.dma(xout[:, :, SEQ:SEQ + NS], xb_[:, :, NPB:NPB + NS])

    if STOP <= 7:
        kb.finish(); return nc, kb
    xfin = xbuf[L % 2]
    with kb.phase():
        xf = [sb("xf%d" % i, [128, 8, 512]) for i in range(2)]
        yf = sb("yf", [128, 8, 512]); yt = [sb("yt%d" % i, [128, D]) for i in range(2)]
        for bi_, (t0, n) in enumerate(tiles(0, NTOT)):
            xx = xf[bi_ % 2]
            kb.dma(xx[:, :, 0:n], xfin[:, :, t0:t0 + n])
            rmsnorm(lambda c, a, m: xx[:, c, a:a + m], lnfc, n, lambda c, a, m: yf[:, c, a:a + m])
            for j in range((n + 127) // 128):
                rows = min(128, n - j * 128)
                y_ = yt[j % 2]
                for c in range(8):
                    pp = pC if c % 2 == 0 else pD
                    kb.tr(pp[0:rows, 0:128], yf[:, c, j * 128:j * 128 + rows], idf[:])
                    if c % 2 == 0:
                        kb.cp(y_[0:rows, c * 128:(c + 1) * 128], pp[0:rows, 0:128])
                    else:
                        kb.act(y_[0:rows, c * 128:(c + 1) * 128], pp[0:rows, 0:128], AF.Copy)
                g0 = t0 + j * 128
                if g0 < SEQ:
                    kb.dma(yp[g0:g0 + rows, :], y_[0:rows, :])
                else:
                    kb.dma(ys[g0 - SEQ:g0 - SEQ + rows, :], y_[0:rows, :])
    kb.finish()
    return nc, kb


def _consts():
    i = np.arange(128)
    c = {}
    c["c_id"] = np.eye(128, dtype=np.float32)
    q = i[:, None]; j = i[None, :]
    prev = np.where(j >= q, 0.0, NEG); cur = np.where(j <= q, 0.0, NEG)
    c["c_mp"] = np.concatenate([prev, cur], 1).astype(np.float32)
    c["c_mp0"] = np.concatenate([np.full((128, 128), NEG), cur], 1).astype(np.float32)
    qi = np.arange(4)[:, None]
    c["c_ms"] = np.concatenate([np.where(np.arange(128)[None, :] >= qi, 0.0, NEG), np.where(np.arange(4)[None, :] <= qi, 0.0, NEG)], 1).astype(np.float32)
    for nm, n, blk in (("", 128, 64), ("4", 64, 4)):
        a = np.arange(n)
        same = (a[:, None] // blk) == (a[None, :] // blk)
        c["c_su" + nm] = (same & (a[:, None] < a[None, :])).astype(np.float32)
        c["c_sl" + nm] = (same & (a[:, None] > a[None, :])).astype(np.float32)
        c["c_iu" + nm] = (same & (a[:, None] <= a[None, :])).astype(np.float32)
        c["c_bo" + nm] = same.astype(np.float32)
        c["c_mniu" + nm] = np.where(same & (a[:, None] <= a[None, :]), 0.0, NEG).astype(np.float32)
        c["c_mnil" + nm] = np.where(same & (a[:, None] >= a[None, :]), 0.0, NEG).astype(np.float32)
    t = np.arange(64)
    bi = ((t[:, None] // 4) == np.arange(16)[None, :]).astype(np.float32)
    c["c_bi"] = bi
    c["c_bm"] = np.broadcast_to(bi.T.reshape(1, 16 * 64), (128, 16 * 64)).astype(np.float32).copy()
    return c


_CACHE = {}


def kernel(x_prompt, x_sample, cache_swa_k, cache_swa_v, state_sc_conv, state_dn_conv, state_dn, state_ffn_conv,
           ln1, w_in, attn_sinks, sc_conv_w, dn_conv_w, dn_A_log, dn_dt_bias, dn_norm_w,
           w_br_a, w_br_b, w_br_c, w_o, ln2, ffn_up, ffn_conv_w, ffn_down, ln_f):
    f = lambda a: np.ascontiguousarray(np.asarray(a, dtype=np.float32))
    x_prompt = f(x_prompt); x_sample = f(x_sample)
    Bp, SEQ, _ = x_prompt.shape
    L = int(np.asarray(ln1).shape[0])
    NSEQS = x_sample.shape[0]
    key = (SEQ, L)
    if key not in _CACHE:
        _CACHE[key] = build(SEQ, L)[0]
    nc = _CACHE[key]
    consts = _consts()
    shared = dict(ln1=f(ln1).reshape(L, 8, 128), w_in=f(w_in), sinks=f(attn_sinks), scw=f(sc_conv_w), dncw=f(dn_conv_w),
                  alog=f(dn_A_log), dtb=f(dn_dt_bias), dnw=f(dn_norm_w), wba=f(w_br_a), wbb=f(w_br_b), wbc=f(w_br_c), wo=f(w_o),
                  ln2=f(ln2).reshape(L, 8, 128), fup=f(ffn_up), fcw=f(ffn_conv_w), fdn=f(ffn_down), lnf=f(ln_f).reshape(8, 128))
    shared.update(consts)
    ck = f(cache_swa_k).reshape(L, NSEQS, 128, 128); cv = f(cache_swa_v).reshape(L, NSEQS, 128, 128)
    ssc = f(state_sc_conv); sdc = f(state_dn_conv); sdn = f(state_dn); sff = f(state_ffn_conv)
    in_maps = []
    for c in range(NCORE):
        sl_ = slice(c * NSQ, (c + 1) * NSQ)
        m = dict(shared)
        m["xp"] = x_prompt[c % Bp]
        m["xsm"] = np.ascontiguousarray(x_sample[sl_].reshape(NS, D))
        m["ck"] = np.ascontiguousarray(ck[:, sl_]); m["cv"] = np.ascontiguousarray(cv[:, sl_])
        m["st_sc"] = np.ascontiguousarray(ssc[:, sl_].reshape(L, NSQ * 2, 512))
        m["st_dnc"] = np.ascontiguousarray(sdc[:, sl_].reshape(L, NSQ * 3, 1536))
        m["st_dn"] = np.ascontiguousarray(sdn[:, sl_])
        m["st_ff"] = np.ascontiguousarray(sff[:, sl_].reshape(L, NSQ * 2, 2 * DFF))
        in_maps.append(m)
    res = run_bass_kernel_spmd(nc, in_maps, core_ids=list(range(NCORE)))
    R = res.results
    cat = lambda k, ax=1: np.concatenate([np.asarray(R[c][k]) for c in range(NCORE)], axis=ax)
    stk = lambda k: np.stack([np.asarray(R[c][k]) for c in range(Bp)], axis=1)
    y_prompt = np.stack([np.asarray(R[c]["yp"]) for c in range(Bp)], 0)
    y_sample = np.concatenate([np.asarray(R[c]["ys"]).reshape(NSQ, 4, D) for c in range(NCORE)], 0)
    kp = stk("o_kp").reshape(L, Bp, 128, 2, 64); vp_ = stk("o_vp").reshape(L, Bp, 128, 2, 64)
    scp = stk("o_scp"); dncp = stk("o_dncp"); dnp = stk("o_dnp"); ffp = stk("o_ffp")
    ks = cat("o_ks").reshape(L, NSEQS, 128, 2, 64); vs = cat("o_vs").reshape(L, NSEQS, 128, 2, 64)
    scs = cat("o_scs"); dncs = cat("o_dncs"); dns = cat("o_dns"); ffs = cat("o_ffs")
    outs = (y_prompt, y_sample, kp, vp_, scp, dncp, dnp, ffp, ks, vs, scs, dncs, dns, ffs)
    return tuple(np.ascontiguousarray(o, dtype=np.float32) for o in outs)
```
